# Optimizing a Trainium2 kernel written in Bass

```python
import math
import jax
import jax.numpy as jnp
from jax import lax
import numpy as np

D_MODEL = 2048
BATCH = 4
SEQ = 4096
DEPTH = 4
DEC_BATCH = 16
DEC_SEQ = 64
PAST_LEN = 4096

CHUNK = 64
D_CONV = D_MODEL // 2
D_RWKV = D_MODEL // 2
HEAD_DIM = 64
N_RWKV_HEADS = D_RWKV // HEAD_DIM
CONV_WIDTH = 3
DECAY_LORA = 64
AAA_LORA = 64
GATE_LORA = 128
D_FF = 4 * D_MODEL
RMS_EPS = 1e-6
GN_EPS = 64e-5
DECAY_SCALE = math.exp(-0.5)

OFF_CIN = 0
OFF_CB = OFF_CIN + D_CONV
OFF_CC = OFF_CB + D_CONV
OFF_R = OFF_CC + D_CONV
OFF_K = OFF_R + D_RWKV
OFF_V = OFF_K + D_RWKV
OFF_WL = OFF_V + D_RWKV
OFF_AL = OFF_WL + DECAY_LORA
OFF_GL = OFF_AL + AAA_LORA
OFF_GATE_CONV = OFF_GL + GATE_LORA
OFF_GATE_RWKV = OFF_GATE_CONV + D_MODEL
D_IN = OFF_GATE_RWKV + D_MODEL
D_SHIFT = OFF_GATE_CONV - OFF_R

kernel_name = "hybrid_shortconv_rwkv7_stream_step"


def rmsnorm(x, g):
    xf = x.astype(jnp.float32)
    out = xf * lax.rsqrt(jnp.mean(xf * xf, axis=-1, keepdims=True) + RMS_EPS) * g.astype(jnp.float32)
    return out.astype(x.dtype)


def wkv_scan(S0, r, w, k, kk, a, v):
    xs = tuple(jnp.moveaxis(t, 1, 0) for t in (r, w, k, kk, a, v))

    def step(S, inp):
        r_t, w_t, k_t, kk_t, a_t, v_t = inp
        sa = jnp.einsum('bhvk,bhk->bhv', S, kk_t)
        S = (S * w_t[:, :, None, :]
             - sa[..., None] * (kk_t * a_t)[:, :, None, :]
             + v_t[..., None] * k_t[:, :, None, :])
        y = jnp.einsum('bhvk,bhk->bhv', S, r_t)
        return S, y

    S_final, ys = lax.scan(step, S0, xs)
    return jnp.moveaxis(ys, 0, 1), S_final


def layer(x, conv_state, shift_state, wkv_state,
          norm1, w_in, mu_shift, conv_w, w0, w2, a0, a2, g2, k_k, k_a, r_k,
          ln_x_w, ln_x_b, w_out_conv, w_out_rwkv, w_o, norm2, w_up, w_down):
    Bsz, T, _ = x.shape
    h = rmsnorm(x, norm1)
    P = h @ w_in

    c_in = P[..., OFF_CIN:OFF_CB]
    c_b = P[..., OFF_CB:OFF_CC]
    c_c = P[..., OFF_CC:OFF_R]
    u = c_c * c_in
    u_pad = jnp.concatenate([conv_state.astype(u.dtype), u], axis=1)
    conv = sum(conv_w[j] * u_pad[:, j:j + T] for j in range(CONV_WIDTH))
    y_conv = (c_b * conv) @ w_out_conv
    new_conv = u_pad[:, T:]

    p_cur = P[..., OFF_R:OFF_GATE_CONV]
    p_first = shift_state.astype(h.dtype)[:, None, :] @ w_in[:, OFF_R:OFF_GATE_CONV]
    p_prev = jnp.concatenate([p_first, p_cur[:, :-1]], axis=1)
    p_mix = p_cur + mu_shift * (p_prev - p_cur)
    new_shift = h[:, -1]

    r = p_mix[..., OFF_R - OFF_R:OFF_K - OFF_R]
    k = p_mix[..., OFF_K - OFF_R:OFF_V - OFF_R]
    v = p_mix[..., OFF_V - OFF_R:OFF_WL - OFF_R]
    xw = p_mix[..., OFF_WL - OFF_R:OFF_AL - OFF_R]
    xa = p_mix[..., OFF_AL - OFF_R:OFF_GL - OFF_R]
    xg = p_mix[..., OFF_GL - OFF_R:]

    d = (w0 + jnp.tanh(xw) @ w2).astype(jnp.float32)
    w = jnp.exp(-DECAY_SCALE * jax.nn.sigmoid(d))
    a = jax.nn.sigmoid((a0 + xa @ a2).astype(jnp.float32))
    g = jax.nn.sigmoid(xg) @ g2

    heads = lambda t: t.astype(jnp.float32).reshape(Bsz, T, N_RWKV_HEADS, HEAD_DIM)
    rf, kf, vf, wf, af = heads(r), heads(k), heads(v), heads(w), heads(a)
    k_k_h = k_k.astype(jnp.float32).reshape(N_RWKV_HEADS, HEAD_DIM)
    k_a_h = k_a.astype(jnp.float32).reshape(N_RWKV_HEADS, HEAD_DIM)
    kk = kf * k_k_h
    kk = kk / jnp.maximum(jnp.sqrt(jnp.sum(kk * kk, axis=-1, keepdims=True)), 1e-12)
    kf = kf * (1.0 + (af - 1.0) * k_a_h)

    y, S_new = wkv_scan(wkv_state.astype(jnp.float32), rf, wf, kf, kk, af, vf)
    mean = jnp.mean(y, axis=-1, keepdims=True)
    var = jnp.mean(jnp.square(y - mean), axis=-1, keepdims=True)
    yn = ((y - mean) * lax.rsqrt(var + GN_EPS)).reshape(Bsz, T, D_RWKV)
    yn = yn * ln_x_w.astype(jnp.float32) + ln_x_b.astype(jnp.float32)
    bonus = jnp.sum(rf * kf * r_k.astype(jnp.float32), axis=-1, keepdims=True) * vf
    y_r = (yn + bonus.reshape(Bsz, T, D_RWKV)).astype(x.dtype) * g
    y_rwkv = y_r @ w_out_rwkv

    gate_c = jax.nn.sigmoid(P[..., OFF_GATE_CONV:OFF_GATE_RWKV])
    gate_r = jax.nn.sigmoid(P[..., OFF_GATE_RWKV:])
    x = x + (gate_c * y_conv + gate_r * y_rwkv) @ w_o

    h2 = rmsnorm(x, norm2)
    x = x + jnp.square(jax.nn.relu(h2 @ w_up)) @ w_down
    return x, new_conv, new_shift, S_new.astype(wkv_state.dtype)


def trunk(x, conv_states, shift_states, wkv_states,
          norm1, w_in, mu_shift, conv_w, w0, w2, a0, a2, g2, k_k, k_a, r_k,
          ln_x_w, ln_x_b, w_out_conv, w_out_rwkv, w_o, norm2, w_up, w_down, norm_f):
    convs, shifts, wkvs = [], [], []
    for l in range(DEPTH):
        x, c, s, S = layer(x, conv_states[l], shift_states[l], wkv_states[l],
                           norm1[l], w_in[l], mu_shift[l], conv_w[l], w0[l], w2[l], a0[l], a2[l],
                           g2[l], k_k[l], k_a[l], r_k[l], ln_x_w[l], ln_x_b[l], w_out_conv[l],
                           w_out_rwkv[l], w_o[l], norm2[l], w_up[l], w_down[l])
        convs.append(c)
        shifts.append(s)
        wkvs.append(S)
    return rmsnorm(x, norm_f), jnp.stack(convs), jnp.stack(shifts), jnp.stack(wkvs)


def setup_inputs(seed: int = 0) -> dict:
    key = jax.random.key(seed)
    ks = jax.random.split(key, 32)
    nrm = lambda k, shape, s: s * jax.random.normal(k, shape, jnp.float32)
    L = DEPTH
    return {
        "x_prompt": nrm(ks[0], (BATCH, SEQ, D_MODEL), 1.0),
        "x_sample": nrm(ks[1], (DEC_BATCH, DEC_SEQ, D_MODEL), 1.0),
        "cache_conv": nrm(ks[2], (L, DEC_BATCH, CONV_WIDTH - 1, D_CONV), 1.0),
        "state_shift": nrm(ks[3], (L, DEC_BATCH, D_MODEL), 1.0),
        "state_wkv": nrm(ks[4], (L, DEC_BATCH, N_RWKV_HEADS, HEAD_DIM, HEAD_DIM), 0.5),
        "norm1": 1.0 + nrm(ks[5], (L, D_MODEL), 0.02),
        "w_in": nrm(ks[6], (L, D_MODEL, D_IN), D_MODEL ** -0.5),
        "mu_shift": jax.random.uniform(ks[7], (L, D_SHIFT), jnp.float32),
        "conv_w": nrm(ks[8], (L, CONV_WIDTH, D_CONV), CONV_WIDTH ** -0.5),
        "w0": nrm(ks[9], (L, D_RWKV), 0.5),
        "w2": nrm(ks[10], (L, DECAY_LORA, D_RWKV), DECAY_LORA ** -0.5),
        "a0": nrm(ks[11], (L, D_RWKV), 0.1),
        "a2": nrm(ks[12], (L, AAA_LORA, D_RWKV), AAA_LORA ** -0.5),
        "g2": nrm(ks[13], (L, GATE_LORA, D_RWKV), GATE_LORA ** -0.5),
        "k_k": 0.85 + nrm(ks[14], (L, D_RWKV), 0.02),
        "k_a": 1.0 + nrm(ks[15], (L, D_RWKV), 0.02),
        "r_k": nrm(ks[16], (L, N_RWKV_HEADS, HEAD_DIM), 0.1),
        "ln_x_w": 1.0 + nrm(ks[17], (L, D_RWKV), 0.02),
        "ln_x_b": nrm(ks[18], (L, D_RWKV), 0.02),
        "w_out_conv": nrm(ks[19], (L, D_CONV, D_MODEL), D_CONV ** -0.5),
        "w_out_rwkv": nrm(ks[20], (L, D_RWKV, D_MODEL), D_RWKV ** -0.5),
        "w_o": nrm(ks[21], (L, D_MODEL, D_MODEL), D_MODEL ** -0.5),
        "norm2": 1.0 + nrm(ks[22], (L, D_MODEL), 0.02),
        "w_up": nrm(ks[23], (L, D_MODEL, D_FF), D_MODEL ** -0.5),
        "w_down": nrm(ks[24], (L, D_FF, D_MODEL), D_FF ** -0.5),
        "norm_f": 1.0 + nrm(ks[25], (D_MODEL,), 0.02),
    }


def reference(x_prompt, x_sample, cache_conv, state_shift, state_wkv,
              norm1, w_in, mu_shift, conv_w, w0, w2, a0, a2, g2, k_k, k_a, r_k,
              ln_x_w, ln_x_b, w_out_conv, w_out_rwkv, w_o, norm2, w_up, w_down, norm_f):
    dt = x_prompt.dtype
    zero_conv = jnp.zeros((DEPTH, BATCH, CONV_WIDTH - 1, D_CONV), dt)
    zero_shift = jnp.zeros((DEPTH, BATCH, D_MODEL), dt)
    zero_wkv = jnp.zeros((DEPTH, BATCH, N_RWKV_HEADS, HEAD_DIM, HEAD_DIM), state_wkv.dtype)
    y_prompt, conv_p, shift_p, wkv_p = trunk(
        x_prompt, zero_conv, zero_shift, zero_wkv,
        norm1, w_in, mu_shift, conv_w, w0, w2, a0, a2, g2, k_k, k_a, r_k,
        ln_x_w, ln_x_b, w_out_conv, w_out_rwkv, w_o, norm2, w_up, w_down, norm_f)
    y_sample, conv_s, shift_s, wkv_s = trunk(
        x_sample, cache_conv, state_shift, state_wkv,
        norm1, w_in, mu_shift, conv_w, w0, w2, a0, a2, g2, k_k, k_a, r_k,
        ln_x_w, ln_x_b, w_out_conv, w_out_rwkv, w_o, norm2, w_up, w_down, norm_f)
    return (y_prompt, y_sample, conv_p, shift_p, wkv_p, conv_s, shift_s, wkv_s)
```

```python
import contextlib
import math
import numpy as np
import concourse.bass as bass
import concourse.mybir as mybir
from concourse.bass_utils import run_bass_kernel_spmd

F32 = mybir.dt.float32
BF16 = mybir.dt.bfloat16
AF = mybir.ActivationFunctionType
ALU = mybir.AluOpType

D = 2048
KC = 16
DCONV = 1024
DRW = 1024
NH = 16
HD = 64
DIN = 10496
DFF = 8192
DEPTH_FULL = 4
RMS_EPS = 1e-6
GN_EPS = 64e-5
DS = math.exp(-0.5)
OFF_R = 3072
OFF_LORA = 6144
OFF_GC = 6400
OFF_GR = 8448
TT = 512
SAME_ENGINE_NOSYNC = False
NODMA = False
GEN_REPS = (1, 1)
LOCK_W = 8
CH = 64
VC_N1 = 0
VC_N2 = 16
VC_MU = 32
VC_CW = 58
VC_W0 = 82
VC_A0 = 90
VC_KK = 98
VC_KA = 106
VC_RK = 114
VC_LW = 122
VC_LB = 130
NVC = 138


class Eng:
    def __init__(self, name, h, sem, is_pe=False):
        self.name, self.h, self.sem, self.is_pe = name, h, sem, is_pe
        self.count = 0
        self.waited = {}


class Chan:
    def __init__(self, name, sem):
        self.name, self.sem = name, sem
        self.count = 0


class Buf:
    __slots__ = ("w", "r", "name", "excl")

    def __init__(self, name="", excl=False):
        self.w = None
        self.r = {}
        self.name = name
        self.excl = excl


class Cfg:
    def __init__(self, npt=8, depth=4, sample=True, do_mix=True, do_mlp=True, do_rwkv=True):
        self.npt, self.depth, self.sample = npt, depth, sample
        self.do_mix, self.do_mlp, self.do_rwkv = do_mix, do_mlp, do_rwkv
        self.dbg = 0
        self.skip = set()
        self.ntok = npt * TT + (128 if sample else 0)


class KB:
    def __init__(self, cfg):
        self.cfg = cfg
        self.nc = nc = bass.Bass("TRN2", target_bir_lowering=False)
        self.es = contextlib.ExitStack()
        self.nsem = 0
        L = cfg.depth
        NT = cfg.ntok
        di = lambda n, s: nc.dram_tensor(n, list(s), F32, kind="ExternalInput").ap()
        do = lambda n, s: nc.dram_tensor(n, list(s), F32, kind="ExternalOutput").ap()
        self.xT_d = di("xT", (D, NT))
        self.vec_d = di("vec", (L, 128, NVC))
        self.nf_d = di("nf", (128, 16))
        self.w_in_d = di("w_in", (L, D, DIN))
        self.w2_d = di("w2", (L, 64, DRW))
        self.a2_d = di("a2", (L, 64, DRW))
        self.g2_d = di("g2", (L, 128, DRW))
        self.woc_d = di("w_out_conv", (L, DCONV, D))
        self.wor_d = di("w_out_rwkv", (L, DRW, D))
        self.wo_d = di("w_o", (L, D, D))
        self.wup_d = di("w_up", (L, D, DFF))
        self.wdn_d = di("w_down", (L, DFF, D))
        self.cst_d = di("cst", (128, NCST))
        self.hs0_d = di("hs0", (L, 128, 16, 2))
        self.cv0_d = di("cv0", (L, 128, 8, 2, 2))
        self.H0_d = di("H0", (L, 2, 8, 128, 64))
        self.yT_d = do("yT", (D, NT))
        self.convo_d = do("convo", (L, 3, 128, 8, 2))
        self.shifto_d = do("shifto", (L, 3, 128, 16))
        self.wkvo_d = do("wkvo", (L, 3, 8, 128, 64))

        ds = lambda n, s: nc.dram_tensor(n, list(s), BF16, kind="Internal").ap()
        self.w_in_s = ds("w_in_s", (L, D, DIN))
        self.w2_s = ds("w2_s", (L, 64, DRW))
        self.a2_s = ds("a2_s", (L, 64, DRW))
        self.g2_s = ds("g2_s", (L, 128, DRW))
        self.woc_s = ds("woc_s", (L, DCONV, D))
        self.wor_s = ds("wor_s", (L, DRW, D))
        self.wo_s = ds("wo_s", (L, D, D))
        self.wup_s = ds("wup_s", (L, D, DFF))
        self.wdn_s = ds("wdn_s", (L, DFF, D))
        self.cvb = {}
        sem = self.new_sem
        self.pe = Eng("pe", nc.tensor, sem("pe"), is_pe=True)
        self.act = Eng("act", nc.scalar, sem("act"))
        self.dve = Eng("dve", nc.vector, sem("dve"))
        self.pool = Eng("pool", nc.gpsimd, sem("pool"))
        self.sp = Eng("sp", nc.sync, sem("sp"))
        self.out_chans = []

    def new_sem(self, name):
        self.nsem += 1
        return self.es.enter_context(self.nc.semaphore(f"s{self.nsem}_{name}"))

    def chan(self, name):
        return Chan(name, self.new_sem("c_" + name))

    def sb(self, name, shape, dt):
        return self.es.enter_context(self.nc.sbuf_tensor(name, list(shape), dt))

    def _deps(self, eng, reads, writes):
        deps = {}

        def need(st):
            if st is None:
                return
            o, v = st
            if deps.get(o, 0) < v:
                deps[o] = v
        for b in reads:
            need(b.w)
            if b.excl:
                for o, v in b.r.items():
                    if o is not eng:
                        need((o, v))
        for b in writes:
            need(b.w)
            for o, v in b.r.items():
                need((o, v))
        for o, v in deps.items():
            if o is eng and (eng.is_pe or SAME_ENGINE_NOSYNC):
                continue
            if eng.waited.get(o, 0) >= v:
                continue
            assert v <= o.count, (eng.name, o.name, v, o.count)
            eng.h.wait_ge(o.sem, v)
            eng.waited[o] = v

    def _stamp(self, st, reads, writes):
        o, v = st
        for b in reads:
            if b.r.get(o, 0) < v:
                b.r[o] = v
        for b in writes:
            b.w = st
            b.r = {}

    def op(self, eng, fn, reads=(), writes=(), signal=True):
        self._deps(eng, reads, writes)
        ins = fn(eng.h)
        if signal:
            eng.count += 1
            ins.then_inc(eng.sem, 1)
            st = (eng, eng.count)
        else:
            st = (eng, eng.count + 1)
        self._stamp(st, reads, writes)
        return ins

    def dma(self, eng, ch, out, in_, reads=(), writes=(), **kw):
        self._deps(eng, reads, writes)
        ins = eng.h.dma_start(out=out, in_=in_, **kw)
        ch.count += 16
        ins.then_inc(ch.sem, 16)
        self._stamp((ch, ch.count), reads, writes)
        return ins

    def bank(self):
        i = self.bank_i
        while i in self.reserved:
            i = (i + 1) % 8
        self.bank_i = (i + 1) % 8
        return self.ps[i], self.psb[i]

    def alloc(self):
        nc = self.nc
        sb = self.sb
        L = self.cfg.depth
        self.xT = sb("xTs", (128, KC, TT), F32)
        self.xT_b = [Buf(f"xT{k}") for k in range(KC)]
        self.hT = sb("hTs", (128, KC, TT + 4), BF16)
        self.hT_b = [Buf(f"hT{k}") for k in range(KC)]
        self.hTx_b = Buf("hTx")
        self.upT = sb("upTs", (128, KC, TT), BF16)
        self.upT_b = [Buf(f"upT{k}") for k in range(KC)]
        self.wt = [sb(f"wt{i}", (128, 8192), BF16) for i in range(2)]
        self.wt_b = [Buf(f"wt{i}") for i in range(2)]
        self.wt_ch = [self.chan(f"wt{i}") for i in range(2)]
        self.wt_i = 0
        self.cst = sb("csts", (128, NCST), F32)
        self.cst_b = Buf("cst")
        self.vec = sb("vecs", (128, L, NVC), F32)
        self.vec_b = Buf("vec")
        self.nf = sb("nfs", (128, 16), F32)
        self.identb = sb("identb", (128, 128), BF16)
        self.eps_rms = sb("eps_rms", (128, 1), F32)
        self.eps_gn = sb("eps_gn", (128, 1), F32)
        self.misc_b = Buf("misc")
        self.ps = [self.es.enter_context(nc.psum_tensor(f"ps{i}", [128, 512], F32)) for i in range(8)]
        self.psb = [Buf(f"ps{i}", excl=True) for i in range(8)]
        self.bank_i = 0
        self.reserved = set()
        self.norm_state = None
        self.ld_ch = self.chan("ld")
        self.xld_ch = self.chan("xld")
        self.yst_ch = self.chan("yst")
        self.out_chans.append(self.yst_ch)
        NP = 14
        self.tp = [sb(f"tp{i}", (128, TT + 4), F32) for i in range(NP)]
        self.tp_b = [Buf(f"tp{i}") for i in range(NP)]
        self.tp_free = list(range(NP))
        self.ycinT = sb("ycinT", (128, 8, TT), BF16)
        self.ycin_b = [Buf(f"ycin{k}") for k in range(8)]
        self.yrT = sb("yrT", (128, 8, TT), BF16)
        self.yr_b = [Buf(f"yr{k}") for k in range(8)]
        self.lw = sb("lw", (128, 2048), BF16)
        self.lw_b = Buf("lw")
        self.lw_ch = self.chan("lw")
        self.lwin = sb("lwin", (128, TT + 2), BF16)
        self.lwin_b = Buf("lwin")
        self.sgb = sb("sgb", (128, TT), BF16)
        self.sgb_b = Buf("sgb")
        self.plast = sb("plast", (128, L, 26), F32)
        self.plast_b = Buf("plast")
        self.ulast = sb("ulast", (128, L, 8, 2), F32)
        self.ulast_b = Buf("ulast")
        self.hs0 = sb("hs0s", (128, L, 16, 2), F32)
        self.cv0 = sb("cv0s", (128, L, 8, 2, 2), F32)
        self.st_b = Buf("states")
        self.rq = sb("rq", (128, 2, TT), BF16)
        self.kt = sb("kt", (128, TT), BF16)
        self.bt = sb("bt", (128, TT), BF16)
        self.ktp = sb("ktp", (128, TT), BF16)
        self.btp = sb("btp", (128, TT), BF16)
        self.vb = sb("vb", (128, TT), BF16)
        self.rhat2 = [sb(f"rhat{i}", (128, TT), BF16) for i in range(2)]
        self.rhat2_b = [Buf(f"rhat{i}") for i in range(2)]
        self.gam2 = [sb(f"gam{i}", (128, 8), F32) for i in range(2)]
        self.gam2_b = [Buf(f"gam{i}") for i in range(2)]
        self.slab_b = {n: Buf(n) for n in ("rq", "kt", "bt", "ktp", "btp", "vb")}
        self.tm = [sb(f"tm{i}", (128, 4, 128), BF16) for i in range(4)]
        self.tm_b = [Buf(f"tm{i}") for i in range(4)]
        self.tmm = [sb(f"tmm{i}", (128, 2, 2, 128), BF16) for i in range(4)]
        self.tmm_b = [Buf(f"tmm{i}") for i in range(4)]
        self.NmG = [sb(f"NmG{i}", (128, 512), BF16) for i in range(8)]
        self.NmG_b = [Buf(f"NmG{i}") for i in range(8)]
        self.XXG = [sb(f"XXG{i}", (128, 2, 2, 128), BF16) for i in range(8)]
        self.XXG_b = [[Buf(f"XXG{i}_{j}") for j in range(2)] for i in range(8)]
        self.ZbG = [sb(f"ZbG{i}", (128, 2, 128), BF16) for i in range(8)]
        self.ZbG_b = [[Buf(f"ZbG{i}_{j}") for j in range(2)] for i in range(8)]
        self.set_i = 0
        self.PTbd = sb("PTbd", (128, 8, 128), BF16)
        self.PT_b = [Buf(f"PT{c}") for c in range(8)]
        self.Gs = sb("Gs", (128, 8, 64), F32)
        self.G_b = [Buf(f"G{c}") for c in range(8)]
        self.Hst = sb("Hst", (128, L, 8, 64), F32)
        self.H_b = [[Buf(f"H{l}_{s}") for s in range(8)] for l in range(L)]
        self.Hs = sb("Hs", (128, 2, 64), F32)
        self.Hs_b = [Buf("Hs0"), Buf("Hs1")]
        self.Hs_ch = [self.chan("Hs0"), self.chan("Hs1")]
        self.Hb = sb("Hb", (128, 64), BF16)
        self.Hb_b = Buf("Hb")
        self.Hbd = sb("Hbd", (128, 128), BF16)
        self.Hbd_b = Buf("Hbd")
        self.Hn = sb("Hn", (128, 64), F32)
        self.Hn_b = Buf("Hn")
        self.ostage = sb("ostage", (128, 48), F32)
        self.ost_b = Buf("ostage")
        self.ost_ch = self.chan("ost")
        self.out_chans.append(self.ost_ch)
        self.wkv_ch = [self.chan(f"wkvo{i}") for i in range(3)]
        self.out_chans.extend(self.wkv_ch)

    def talloc(self):
        i = self.tp_free.pop(0)
        return i

    def tfree(self, *idx):
        for i in idx:
            assert i not in self.tp_free
            self.tp_free.append(i)

    def A(self, out, in_, func, r, w, bias=None, scale=None):
        kw = {}
        if bias is not None:
            kw["bias"] = bias
        if scale is not None:
            kw["scale"] = scale
        return self.op(self.act, lambda e: e.activation(out=out, in_=in_, func=func, **kw), reads=r, writes=w)

    def Vtt(self, out, in0, in1, op, r, w):
        return self.op(self.dve, lambda e: e.tensor_tensor(out=out, in0=in0, in1=in1, op=op), reads=r, writes=w)

    def Vts(self, out, in0, s1, op0, r, w, s2=None, op1=None):
        if op1 is None:
            return self.op(self.dve, lambda e: e.tensor_scalar(out=out, in0=in0, scalar1=s1, scalar2=None, op0=op0), reads=r, writes=w)
        return self.op(self.dve, lambda e: e.tensor_scalar(out=out, in0=in0, scalar1=s1, scalar2=s2, op0=op0, op1=op1), reads=r, writes=w)

    def Vstt(self, out, in0, scalar, in1, op0, op1, r, w):
        return self.op(self.dve, lambda e: e.scalar_tensor_tensor(out=out, in0=in0, scalar=scalar, in1=in1, op0=op0, op1=op1), reads=r, writes=w)

    def Vcp(self, out, in_, r, w):
        return self.op(self.dve, lambda e: e.tensor_copy(out=out, in_=in_), reads=r, writes=w)

    def MM(self, out, lhsT, rhs, r, w, start=True, stop=True, signal=True):
        return self.op(self.pe, lambda e: e.matmul(out, lhsT, rhs, start=start, stop=stop), reads=r, writes=w, signal=signal)

    def TR(self, out, in_, r, w, signal=True):
        return self.op(self.pe, lambda e: e.transpose(out, in_, self.identb[:]), reads=r + [self.misc_b], writes=w, signal=signal)

    def cslice(self, c0, n):
        return self.cst[:, c0:c0 + n]

    def setup(self):
        L = self.cfg.depth
        self.nf_b = Buf("nf")
        self.st2_b = Buf("st2")
        self.dma(self.sp, self.chan("ld1"), self.cst[:], self.cst_d[:, :], writes=[self.cst_b])
        self.dma(self.sp, self.chan("ld2"), self.vec[:], self.vec_d.rearrange("l p c -> p l c"), writes=[self.vec_b])
        self.dma(self.sp, self.chan("ld3"), self.nf[:], self.nf_d[:, :], writes=[self.nf_b])
        self.dma(self.sp, self.chan("ld4"), self.hs0[:], self.hs0_d.rearrange("l p k b -> p l k b"), writes=[self.st_b])
        self.dma(self.sp, self.chan("ld5"), self.cv0[:], self.cv0_d.rearrange("l p c b j -> p l c b j"), writes=[self.st2_b])
        self.op(self.dve, lambda e: e.memset(self.eps_rms[:], RMS_EPS), writes=[self.misc_b])
        self.op(self.dve, lambda e: e.memset(self.eps_gn[:], GN_EPS), writes=[self.misc_b])
        self.ident_f = self.cslice(C_ID, 128)
        self.ones_f = self.cslice(C_ONES, 128)
        self.bo1 = self.cslice(C_BO, 128)
        self.mask4 = self.cslice(C_M4, 512)
        self.mls = self.cslice(C_MLS, 128)
        self.chmask = self.cslice(C_CHM, 512)
        self.rowmask = self.cslice(C_ROWM, 2)
        self.bdmask = self.cslice(C_BDM, 128)
        self.Vcp(self.identb[:], self.ident_f, [self.cst_b], [self.misc_b])
        self.op(self.dve, lambda e: e.memset(self.plast[:], 0.0), writes=[self.plast_b])
        self.op(self.dve, lambda e: e.memset(self.ulast[:], 0.0), writes=[self.ulast_b])
        self.op(self.dve, lambda e: e.memset(self.PTbd[:], 0.0), writes=self.PT_b)
        for l in range(L):
            self.op(self.dve, lambda e: e.memset(self.Hst[:, l], 0.0), writes=self.H_b[l])
        def cv(key, dst, src_):
            b = Buf("cv_" + str(key))
            self.cvb[key] = b
            self.dma(self.pool, self.chan("cv"), dst, src_, writes=[b])
        for l in range(L):
            cv(("win_c", l), self.w_in_s[l][:, 0:3072], self.w_in_d[l][:, 0:3072])
            cv(("win_r", l), self.w_in_s[l][:, 3072:6400], self.w_in_d[l][:, 3072:6400])
            cv(("w2", l), self.w2_s[l], self.w2_d[l])
            cv(("a2", l), self.a2_s[l], self.a2_d[l])
            cv(("g2", l), self.g2_s[l], self.g2_d[l])
            cv(("win_g", l), self.w_in_s[l][:, 6400:DIN], self.w_in_d[l][:, 6400:DIN])
            cv(("woc", l), self.woc_s[l], self.woc_d[l])
            cv(("wor", l), self.wor_s[l], self.wor_d[l])
            cv(("wo", l), self.wo_s[l], self.wo_d[l])
            for g in range(4):
                cv(("wup", l, g), self.wup_s[l][:, g * 2048:(g + 1) * 2048], self.wup_d[l][:, g * 2048:(g + 1) * 2048])
            for g in range(4):
                cv(("wdn", l, g), self.wdn_s[l][g * 2048:(g + 1) * 2048, :], self.wdn_d[l][g * 2048:(g + 1) * 2048, :])

    def load_x(self, tok0, T):
        src = self.xT_d.rearrange("(kc p) t -> p kc t", p=128)[:, :, tok0:tok0 + T]
        self.dma(self.sp, self.xld_ch, self.xT[:, :, :T], src, writes=self.xT_b)

    def norm_begin(self, T):
        i = self.bank_i
        while i in self.reserved:
            i = (i + 1) % 8
        self.bank_i = (i + 1) % 8
        self.reserved.add(i)
        self.norm_state = {"bank": i, "n": 0, "T": T}

    def norm_kc(self, kc):
        st = self.norm_state
        T = st["T"]
        ps, psb = self.ps[st["bank"]], self.psb[st["bank"]]
        i = self.talloc()
        self.A(self.tp[i][:, :T], self.xT[:, kc, :T], AF.Square, [self.xT_b[kc]], [self.tp_b[i]])
        self.MM(ps[:, :T], self.ones_f, self.tp[i][:, :T], [self.tp_b[i], self.cst_b], [psb], start=(st["n"] == 0), stop=(st["n"] == KC - 1))
        self.tfree(i)
        st["n"] += 1

    def rstd_of_x(self, T):
        if self.norm_state is None:
            self.norm_begin(T)
            for kc in range(KC):
                self.norm_kc(kc)
        st = self.norm_state
        assert st["n"] == KC and st["T"] == T
        ps, psb = self.ps[st["bank"]], self.psb[st["bank"]]
        i = self.talloc()
        self.A(self.tp[i][:, :T], ps[:, :T], AF.Sqrt, [psb, self.misc_b], [self.tp_b[i]], bias=self.eps_rms[:], scale=1.0 / D)
        self.op(self.dve, lambda e: e.reciprocal(out=self.tp[i][:, :T], in_=self.tp[i][:, :T]), reads=[self.tp_b[i]], writes=[self.tp_b[i]])
        self.reserved.discard(st["bank"])
        self.norm_state = None
        return i

    def rmsnorm(self, gsl, dst, dst_b, T):
        i = self.rstd_of_x(T)
        for kc in range(KC):
            self.Vstt(dst[:, kc, :T], self.xT[:, kc, :T], gsl(kc), self.tp[i][:, :T], ALU.mult, ALU.mult,
                      [self.xT_b[kc], self.tp_b[i], self.vec_b], [dst_b[kc]])
        return i

    def wload(self, parts, deps):
        i = self.wt_i
        self.wt_i = 1 - i
        t, b, ch = self.wt[i], self.wt_b[i], self.wt_ch[i]
        if NODMA and ch.count >= 64:
            return t, b
        for dstf, src in parts:
            self.dma(self.sp, ch, dstf(t), src, reads=[self.cvb[k] for k in deps], writes=[b])
        return t, b

    def proj(self, w3, wb, cols, T, rhs, rhs_b, nk=KC):
        ps, psb = self.bank()
        for kc in range(nk):
            self.MM(ps[:, :T], w3[:, kc, cols], rhs[:, kc, :T], [wb] + rhs_b(kc), [psb], start=(kc == 0), stop=(kc == nk - 1), signal=(kc == nk - 1))
        return ps, psb

    def mlp(self, l, T):
        g2 = lambda kc: self.vec[:, l, VC_N2 + kc:VC_N2 + kc + 1]
        i = self.rmsnorm(g2, self.hT, self.hT_b, T)
        self.tfree(i)
        wup = self.wup_s[l].rearrange("(kc p) n -> p kc n", p=128)
        wdn = self.wdn_s[l].rearrange("(g kc p) n -> g p kc n", p=128, kc=16)
        v3 = lambda t: t[:, 0:8192].rearrange("p (kc n) -> p kc n", kc=16)
        hb = lambda kc: [self.hT_b[kc]]
        for g in range(4):
            for u in range(4):
                c0 = (g * 4 + u) * 512
                wt, wb = self.wload([(v3, wup[:, :, c0:c0 + 512])], [("wup", l, g)])
                w3 = v3(wt)
                for j in range(4):
                    ps, psb = self.proj(w3, wb, slice(j * 128, (j + 1) * 128), T, self.hT, hb)
                    ti = self.talloc()
                    self.A(self.tp[ti][:, :T], ps[:, :T], AF.Relu, [psb], [self.tp_b[ti]])
                    f = u * 4 + j
                    self.Vtt(self.upT[:, f, :T], self.tp[ti][:, :T], self.tp[ti][:, :T], ALU.mult, [self.tp_b[ti]], [self.upT_b[f]])
                    self.tfree(ti)
            if g == 3:
                self.norm_begin(T)
            for u in range(4):
                wt, wb = self.wload([(v3, wdn[g, :, :, u * 512:(u + 1) * 512])], [("wdn", l, g)])
                w3 = v3(wt)
                for j in range(4):
                    o = u * 4 + j
                    ps, psb = self.proj(w3, wb, slice(j * 128, (j + 1) * 128), T, self.upT, lambda kc: [self.upT_b[kc]])
                    self.Vtt(self.xT[:, o, :T], self.xT[:, o, :T], ps[:, :T], ALU.add, [psb, self.xT_b[o]], [self.xT_b[o]])
                    if g == 3 and o >= 1:
                        self.norm_kc(o - 1)
            if g == 3:
                self.norm_kc(KC - 1)

    def mix_shift(self, l, q, ps, psb, segs, T, bnd):
        raw = self.talloc()
        dl = self.talloc()
        mu = self.vec[:, l, VC_MU + q:VC_MU + q + 1]
        R = self.tp[raw]
        for si, (c0, n, _) in enumerate(segs):
            base = c0 + si
            self.A(R[:, base + 1:base + 1 + n], ps[:, c0:c0 + n], AF.Copy, [psb], [self.tp_b[raw]])
            bap, bb = bnd(si)
            self.Vcp(R[:, base:base + 1], bap, bb, [self.tp_b[raw]])
        for si, (c0, n, _) in enumerate(segs):
            base = c0 + si
            self.Vtt(self.tp[dl][:, c0:c0 + n], R[:, base:base + n], R[:, base + 1:base + 1 + n], ALU.subtract, [self.tp_b[raw]], [self.tp_b[dl]])
            self.Vstt(self.tp[dl][:, c0:c0 + n], self.tp[dl][:, c0:c0 + n], mu, R[:, base + 1:base + 1 + n], ALU.mult, ALU.add,
                      [self.tp_b[dl], self.tp_b[raw], self.vec_b], [self.tp_b[dl]])
        if not segs[0][2]["sample"]:
            c0, n, _ = segs[0]
            self.Vcp(self.plast[:, l, q:q + 1], R[:, c0 + n:c0 + n + 1], [self.tp_b[raw]], [self.plast_b])
        self.tfree(raw)
        return dl

    def mixer(self, l, T, segs, ti):
        cfg = self.cfg
        is_s = segs[0][2]["sample"]
        nch = T // CH
        npair = T // 128
        win = self.w_in_s[l].rearrange("(kc p) n -> p kc n", p=128)
        V = lambda c: self.vec[:, l, c:c + 1]
        hb = lambda kc: [self.hT_b[kc]]
        self.dma(self.sp, self.lw_ch, self.lw[0:64, 0:1024], self.w2_s[l], reads=[self.cvb[("w2", l)]], writes=[self.lw_b])
        self.dma(self.sp, self.lw_ch, self.lw[64:128, 0:1024], self.a2_s[l], reads=[self.cvb[("a2", l)]], writes=[self.lw_b])
        self.dma(self.sp, self.lw_ch, self.lw[:, 1024:2048], self.g2_s[l], reads=[self.cvb[("g2", l)]], writes=[self.lw_b])
        g1 = lambda kc: self.vec[:, l, VC_N1 + kc:VC_N1 + kc + 1]
        ri = self.rmsnorm(g1, self.hT, self.hT_b, T)
        for si, (c0, n, info) in enumerate(segs):
            if info["last"]:
                last = c0 + n - 1
                for kc in range(KC):
                    self.Vstt(self.ostage[:, kc:kc + 1], self.xT[:, kc, last:last + 1], g1(kc), self.tp[ri][:, last:last + 1], ALU.mult, ALU.mult,
                              [self.xT_b[kc], self.tp_b[ri], self.vec_b], [self.ost_b])
                self.dma(self.sp, self.ost_ch, self.shifto_d[l, info["seq"]], self.ostage[:, 0:16], reads=[self.ost_b])
        self.tfree(ri)
        if is_s:
            for b in range(2):
                self.Vcp(self.hT[:, :, T + b:T + b + 1], self.hs0[:, l, :, b:b + 1], [self.st_b], [self.hTx_b])

        def shift_bnd(q, w3, wb, cols):
            if not is_s:
                return lambda si: (self.plast[:, l, q:q + 1], [self.plast_b])
            ps, psb = self.bank()
            for kc in range(KC):
                self.MM(ps[:, 0:2], w3[:, kc, cols], self.hT[:, kc, T:T + 2], [wb, self.hTx_b], [psb], start=(kc == 0), stop=(kc == KC - 1), signal=(kc == KC - 1))
            return lambda si: (ps[:, si:si + 1], [psb])

        v32 = lambda t: t[:, 0:8192].rearrange("p (kc j n) -> p kc j n", kc=16, j=2)
        v31 = lambda t: t[:, 0:4096].rearrange("p (kc n) -> p kc n", kc=16)
        win_c = self.w_in_s[l][:, 0:3072].rearrange("(kc p) (j n) -> p kc j n", p=128, j=3)
        for q in range(4):
            wtA, wbA = self.wload([(lambda t: v32(t)[:, :, 0, :], win_c[:, :, 0, q * 256:(q + 1) * 256]),
                                   (lambda t: v32(t)[:, :, 1, :], win_c[:, :, 2, q * 256:(q + 1) * 256])], [("win_c", l)])
            wA = v32(wtA)
            convs = []
            for hf in range(2):
                cg = q * 2 + hf
                cols = slice(hf * 128, (hf + 1) * 128)
                ps_in, pb_in = self.proj(wA[:, :, 0, :], wbA, cols, T, self.hT, hb)
                ps_c, pb_c = self.proj(wA[:, :, 1, :], wbA, cols, T, self.hT, hb)
                ub = self.talloc()
                acc = self.talloc()
                tmp = self.talloc()
                U = self.tp[ub]
                self.A(self.tp[tmp][:, :T], ps_in[:, :T], AF.Copy, [pb_in], [self.tp_b[tmp]])
                for si, (c0, n, info) in enumerate(segs):
                    base = c0 + 2 * si
                    self.Vtt(U[:, base + 2:base + 2 + n], self.tp[tmp][:, c0:c0 + n], ps_c[:, c0:c0 + n], ALU.mult, [self.tp_b[tmp], pb_c], [self.tp_b[ub]])
                    if info["sample"]:
                        self.Vcp(U[:, base:base + 2], self.cv0[:, l, cg, info["b"], :], [self.st2_b], [self.tp_b[ub]])
                    else:
                        self.Vcp(U[:, base:base + 2], self.ulast[:, l, cg, :], [self.ulast_b], [self.tp_b[ub]])
                    cw = lambda j: V(VC_CW + j * 8 + cg)
                    AC = self.tp[acc]
                    self.Vts(AC[:, c0:c0 + n], U[:, base:base + n], cw(0), ALU.mult, [self.tp_b[ub], self.vec_b], [self.tp_b[acc]])
                    self.Vstt(AC[:, c0:c0 + n], U[:, base + 1:base + 1 + n], cw(1), AC[:, c0:c0 + n], ALU.mult, ALU.add, [self.tp_b[ub], self.tp_b[acc], self.vec_b], [self.tp_b[acc]])
                    self.Vstt(AC[:, c0:c0 + n], U[:, base + 2:base + 2 + n], cw(2), AC[:, c0:c0 + n], ALU.mult, ALU.add, [self.tp_b[ub], self.tp_b[acc], self.vec_b], [self.tp_b[acc]])
                    if info["sample"] or True:
                        pass
                    if not info["sample"]:
                        self.Vcp(self.ulast[:, l, cg, :], U[:, base + n:base + n + 2], [self.tp_b[ub]], [self.ulast_b])
                    if info["last"]:
                        ob = 16 + si * 16
                        self.Vcp(self.ostage[:, ob + cg * 2:ob + cg * 2 + 2], U[:, base + n:base + n + 2], [self.tp_b[ub]], [self.ost_b])
                        if cg == 7:
                            self.dma(self.sp, self.ost_ch, self.convo_d[l, info["seq"]], self.ostage[:, ob:ob + 16].rearrange("p (c j) -> p c j", j=2), reads=[self.ost_b])
                self.tfree(ub, tmp)
                convs.append((cg, acc))
            wtB, wbB = self.wload([(v31, win[:, :, 1024 + q * 256:1024 + (q + 1) * 256])], [("win_c", l)])
            wB = v31(wtB)
            for hf, (cg, acc) in enumerate(convs):
                cols = slice(hf * 128, (hf + 1) * 128)
                ps_b, pb_b = self.proj(wB, wbB, cols, T, self.hT, hb)
                self.Vtt(self.ycinT[:, cg, :T], self.tp[acc][:, :T], ps_b[:, :T], ALU.mult, [self.tp_b[acc], pb_b], [self.ycin_b[cg]])
                self.tfree(acc)

        if cfg.do_rwkv:
            self.rwkv(l, T, segs, win, shift_bnd)

        woc = self.woc_s[l].rearrange("(kc p) n -> p kc n", p=128)
        wor = self.wor_s[l].rearrange("(kc p) n -> p kc n", p=128)
        v2x8 = lambda t: t[:, 0:4096].rearrange("p (a kc n) -> p a kc n", a=2, kc=8)
        for q in range(8):
            c0 = q * 256
            wtG, wbG = self.wload([(lambda t: v32(t)[:, :, 0, :], win[:, :, OFF_GC + c0:OFF_GC + c0 + 256]),
                                   (lambda t: v32(t)[:, :, 1, :], win[:, :, OFF_GR + c0:OFF_GR + c0 + 256])], [("win_g", l)])
            wG = v32(wtG)
            wtA, wbA = self.wload([(lambda t: v2x8(t)[:, 0], woc[:, :, c0:c0 + 256]), (lambda t: v2x8(t)[:, 1], wor[:, :, c0:c0 + 256])],
                                  [("woc", l), ("wor", l)])
            wA = v2x8(wtA)
            gates = []
            for hf in range(2):
                cols = slice(hf * 128, (hf + 1) * 128)
                ps_gc, pb_gc = self.proj(wG[:, :, 0, :], wbG, cols, T, self.hT, hb)
                t1 = self.talloc()
                self.A(self.tp[t1][:, :T], ps_gc[:, :T], AF.Sigmoid, [pb_gc], [self.tp_b[t1]])
                t2 = None
                if cfg.do_rwkv:
                    ps_gr, pb_gr = self.proj(wG[:, :, 1, :], wbG, cols, T, self.hT, hb)
                    t2 = self.talloc()
                    self.A(self.tp[t2][:, :T], ps_gr[:, :T], AF.Sigmoid, [pb_gr], [self.tp_b[t2]])
                gates.append((t1, t2))
            for hf in range(2):
                o = q * 2 + hf
                cols = slice(hf * 128, (hf + 1) * 128)
                t1, t2 = gates[hf]
                ps_yc, pb_yc = self.proj(wA[:, 0], wbA, cols, T, self.ycinT, lambda kc: [self.ycin_b[kc]], nk=8)
                if cfg.do_rwkv:
                    ps_yr, pb_yr = self.proj(wA[:, 1], wbA, cols, T, self.yrT, lambda kc: [self.yr_b[kc]], nk=8)
                    self.Vtt(self.tp[t1][:, :T], self.tp[t1][:, :T], ps_yc[:, :T], ALU.mult, [self.tp_b[t1], pb_yc], [self.tp_b[t1]])
                    self.Vtt(self.tp[t2][:, :T], self.tp[t2][:, :T], ps_yr[:, :T], ALU.mult, [self.tp_b[t2], pb_yr], [self.tp_b[t2]])
                    self.Vtt(self.upT[:, o, :T], self.tp[t1][:, :T], self.tp[t2][:, :T], ALU.add, [self.tp_b[t1], self.tp_b[t2]], [self.upT_b[o]])
                    self.tfree(t2)
                else:
                    self.Vtt(self.upT[:, o, :T], self.tp[t1][:, :T], ps_yc[:, :T], ALU.mult, [self.tp_b[t1], pb_yc], [self.upT_b[o]])
                self.tfree(t1)
        wo = self.wo_s[l].rearrange("(kc p) n -> p kc n", p=128)
        v3 = lambda t: t[:, 0:8192].rearrange("p (kc n) -> p kc n", kc=16)
        self.norm_begin(T)
        for u in range(4):
            wt, wb = self.wload([(v3, wo[:, :, u * 512:(u + 1) * 512])], [("wo", l)])
            w3 = v3(wt)
            for j in range(4):
                o = u * 4 + j
                ps, psb = self.proj(w3, wb, slice(j * 128, (j + 1) * 128), T, self.upT, lambda kc: [self.upT_b[kc]])
                self.Vtt(self.xT[:, o, :T], self.xT[:, o, :T], ps[:, :T], ALU.add, [psb, self.xT_b[o]], [self.xT_b[o]])
                if o >= 1:
                    self.norm_kc(o - 1)
        self.norm_kc(KC - 1)

    def run_gens(self, gens):
        gens = [g for g in gens if g is not None]
        reps = GEN_REPS
        while gens:
            for gi, g in enumerate(list(gens)):
                try:
                    for _ in range(reps[gi % len(reps)]):
                        next(g)
                except StopIteration:
                    gens.remove(g)

    def rwkv(self, l, T, segs, win, shift_bnd):
        hb = lambda kc: [self.hT_b[kc]]
        tp, tb = self.tp, self.tp_b
        v31 = lambda t: t[:, 0:4096].rearrange("p (kc n) -> p kc n", kc=16)
        wtL, wbL = self.wload([(v31, win[:, :, OFF_LORA:OFF_LORA + 256])], [("win_r", l)])
        wL = v31(wtL)
        cA = slice(0, 128)
        cB = slice(128, 256)
        psA, pbA = self.proj(wL, wbL, cA, T, self.hT, hb)
        mA = self.mix_shift(l, 24, psA, pbA, segs, T, shift_bnd(24, wL, wbL, cA))
        self.A(self.lwin[0:64, :T], tp[mA][0:64, :T], AF.Tanh, [tb[mA]], [self.lwin_b])
        self.A(self.lwin[64:128, :T], tp[mA][64:128, :T], AF.Copy, [tb[mA]], [self.lwin_b])
        self.tfree(mA)
        psB, pbB = self.proj(wL, wbL, cB, T, self.hT, hb)
        mB = self.mix_shift(l, 25, psB, pbB, segs, T, shift_bnd(25, wL, wbL, cB))
        self.A(self.sgb[:, :T], tp[mB][:, :T], AF.Sigmoid, [tb[mB]], [self.sgb_b])
        self.tfree(mB)
        win_r = self.w_in_s[l][:, OFF_R:OFF_R + 3072].rearrange("(kc p) (j n) -> p kc j n", p=128, j=3)
        prev_tail = None
        pctx = None
        for s in range(8):
            ctx = {"A_done": False, "tm_done": False, "prev": pctx}
            pctx = ctx
            head = self.slab_head(l, s, T, segs, win_r, shift_bnd, ctx)
            self.run_gens([prev_tail, head])
            prev_tail = self.slab_tail(l, s, T, segs, ctx)
            if "tail" in self.cfg.skip:
                ctx["A_done"] = ctx["tm_done"] = True
                self.tfree(ctx["bon"])
                prev_tail = None
        self.run_gens([prev_tail])

    def slab_head(self, l, s, T, segs, win_r, shift_bnd, ctx):
        tp, tb = self.tp, self.tp_b
        SB = self.slab_b
        hb = lambda kc: [self.hT_b[kc]]
        V = lambda c: self.vec[:, l, c + s:c + s + 1]
        nch = T // CH
        npair = T // 128
        par = s % 2
        v33 = lambda t: t[:, 0:6144].rearrange("p (kc j n) -> p kc j n", kc=16, j=3)
        wt, wb = self.wload([((lambda t, j=j: v33(t)[:, :, j, :]), win_r[:, :, j, s * 128:(s + 1) * 128]) for j in range(3)], [("win_r", l)])
        w = v33(wt)
        call = slice(0, 128)
        m = []
        for j in range(3):
            q = j * 8 + s
            ps, pb = self.proj(w[:, :, j, :], wb, call, T, self.hT, hb)
            yield
            m.append(self.mix_shift(l, q, ps, pb, segs, T, shift_bnd(q, w[:, :, j, :], wb, call)))
            yield
        r_f, k_f, v_f = m
        ps_d, pb_d = self.bank()
        self.MM(ps_d[:, :T], self.lw[0:64, s * 128:(s + 1) * 128], self.lwin[0:64, :T], [self.lw_b, self.lwin_b], [pb_d])
        ps_a, pb_a = self.bank()
        self.MM(ps_a[:, :T], self.lw[64:128, s * 128:(s + 1) * 128], self.lwin[64:128, :T], [self.lw_b, self.lwin_b], [pb_a])
        a_f = self.talloc()
        self.A(tp[a_f][:, :T], ps_a[:, :T], AF.Sigmoid, [pb_a, self.vec_b], [tb[a_f]], bias=V(VC_A0))
        sg = self.talloc()
        self.A(tp[sg][:, :T], ps_d[:, :T], AF.Sigmoid, [pb_d, self.vec_b], [tb[sg]], bias=V(VC_W0))
        yield
        kkr = self.talloc()
        self.Vts(tp[kkr][:, :T], tp[k_f][:, :T], V(VC_KK), ALU.mult, [tb[k_f], self.vec_b], [tb[kkr]])
        t = self.talloc()
        self.A(tp[t][:, :T], tp[kkr][:, :T], AF.Square, [tb[kkr]], [tb[t]])
        cs = self.talloc()
        self.op(self.dve, lambda e: e.tensor_tensor_scan(out=tp[cs][:, :T], data0=self.chmask[:, :T], data1=tp[sg][:, :T], initial=0.0,
                                                         op0=ALU.mult, op1=ALU.add), reads=[tb[sg], self.cst_b], writes=[tb[cs]])
        yield
        ps_n, pb_n = self.bank()
        self.MM(ps_n[:, :T], self.bo1, tp[t][:, :T], [tb[t], self.cst_b], [pb_n])
        t2 = self.talloc()
        self.Vts(tp[t2][:, :T], tp[a_f][:, :T], -1.0, ALU.add, [tb[a_f], self.vec_b], [tb[t2]], s2=V(VC_KA), op1=ALU.mult)
        kp = self.talloc()
        self.Vstt(tp[kp][:, :T], tp[t2][:, :T], 1.0, tp[k_f][:, :T], ALU.add, ALU.mult, [tb[t2], tb[k_f]], [tb[kp]])
        self.tfree(k_f)
        self.A(tp[t][:, :T], ps_n[:, :T], AF.Sqrt, [pb_n], [tb[t]])
        yield
        self.Vstt(tp[t2][:, :T], tp[r_f][:, :T], V(VC_RK), tp[kp][:, :T], ALU.mult, ALU.mult, [tb[r_f], tb[kp], self.vec_b], [tb[t2]])
        ps_b, pb_b = self.bank()
        self.MM(ps_b[:, :T], self.bo1, tp[t2][:, :T], [tb[t2], self.cst_b], [pb_b])
        eg = self.talloc()
        self.A(tp[eg][:, :T], tp[cs][:, :T], AF.Exp, [tb[cs]], [tb[eg]], scale=-DS)
        self.Vts(tp[t][:, :T], tp[t][:, :T], 1e-12, ALU.max, [tb[t]], [tb[t]])
        yield
        self.op(self.dve, lambda e: e.reciprocal(out=tp[t][:, :T], in_=tp[t][:, :T]), reads=[tb[t]], writes=[tb[t]])
        bon = t2
        self.Vtt(tp[bon][:, :T], tp[v_f][:, :T], ps_b[:, :T], ALU.mult, [tb[v_f], pb_b], [tb[bon]])
        yield
        self.Vtt(tp[kkr][:, :T], tp[kkr][:, :T], tp[t][:, :T], ALU.mult, [tb[kkr], tb[t]], [tb[kkr]])
        self.tfree(t)
        e = self.talloc()
        self.Vtt(tp[e][:, :T], tp[cs][:, :T], tp[sg][:, :T], ALU.subtract, [tb[cs], tb[sg]], [tb[e]])
        self.A(tp[e][:, :T], tp[e][:, :T], AF.Exp, [tb[e]], [tb[e]], scale=-DS)
        nb = self.talloc()
        self.Vtt(tp[nb][:, :T], tp[kkr][:, :T], tp[a_f][:, :T], ALU.mult, [tb[kkr], tb[a_f]], [tb[nb]])
        self.tfree(a_f)
        yield
        prev = ctx["prev"]
        while prev is not None and not prev["A_done"]:
            yield
        self.Vtt(self.rq[:, 0, :T], tp[kkr][:, :T], tp[e][:, :T], ALU.mult, [tb[kkr], tb[e]], [SB["rq"]])
        self.Vtt(self.rq[:, 1, :T], tp[r_f][:, :T], tp[eg][:, :T], ALU.mult, [tb[r_f], tb[eg]], [SB["rq"]])
        rhat, rhb = self.rhat2[par], self.rhat2_b[par]
        self.Vtt(rhat[:, :T], tp[r_f][:, :T], tp[eg][:, :T], ALU.mult, [tb[r_f], tb[eg]], [rhb])
        gam, gamb = self.gam2[par], self.gam2_b[par]
        self.Vcp(gam[:, 0:nch], tp[eg][:, :T].rearrange("p (c t) -> p c t", t=CH)[:, :, CH - 1], [tb[eg]], [gamb])
        self.tfree(eg)
        yield
        self.A(tp[e][:, :T], tp[cs][:, :T], AF.Exp, [tb[cs]], [tb[e]], scale=DS)
        self.Vtt(self.kt[:, :T], tp[kp][:, :T], tp[e][:, :T], ALU.mult, [tb[kp], tb[e]], [SB["kt"]])
        self.Vstt(self.bt[:, :T], tp[nb][:, :T], -1.0, tp[e][:, :T], ALU.mult, ALU.mult, [tb[nb], tb[e]], [SB["bt"]])
        yield
        csv = tp[cs][:, :T].rearrange("p (c t) -> p c t", t=CH)
        self.Vtt(tp[e][:, :T].rearrange("p (c t) -> p c t", t=CH), csv[:, :, CH - 1:CH].broadcast_to([128, nch, CH]), csv, ALU.subtract,
                 [tb[cs]], [tb[e]])
        self.A(tp[e][:, :T], tp[e][:, :T], AF.Exp, [tb[e]], [tb[e]], scale=-DS)
        yield
        self.Vtt(self.ktp[:, :T], tp[kp][:, :T], tp[e][:, :T], ALU.mult, [tb[kp], tb[e]], [SB["ktp"]])
        self.Vstt(self.btp[:, :T], tp[nb][:, :T], -1.0, tp[e][:, :T], ALU.mult, ALU.mult, [tb[nb], tb[e]], [SB["btp"]])
        self.A(self.vb[:, :T], tp[v_f][:, :T], AF.Copy, [tb[v_f]], [SB["vb"]])
        self.tfree(e, cs, sg, kkr, kp, nb, r_f, v_f)
        yield
        while prev is not None and not prev["tm_done"]:
            yield
        for pr in range(npair):
            cols = slice(pr * 128, (pr + 1) * 128)
            ps, psb = self.bank()
            pb16 = ps[:].bitcast(BF16)
            srcs = [(self.rq[:, 0, cols], "rq"), (self.btp[:, cols], "btp"), (self.ktp[:, cols], "ktp"), (self.vb[:, cols], "vb")]
            for j, (ap, nm) in enumerate(srcs):
                self.TR(pb16[:, j * 128:(j + 1) * 128], ap, [SB[nm]], [psb], signal=(j == 3))
            self.A(self.tm[pr][:].rearrange("p a b -> p (a b)"), pb16[:, 0:512], AF.Copy, [psb], [self.tm_b[pr]])
            for c in range(2):
                self.Vts(self.tmm[pr][:, 0, c, :], self.tm[pr][:, 1, :], self.rowmask[:, c:c + 1], ALU.mult, [self.tm_b[pr], self.cst_b], [self.tmm_b[pr]])
                self.Vts(self.tmm[pr][:, 1, c, :], self.tm[pr][:, 2, :], self.rowmask[:, c:c + 1], ALU.mult, [self.tm_b[pr], self.cst_b], [self.tmm_b[pr]])
            yield
        ctx["bon"] = bon
        ctx["par"] = par

    def slab_tail(self, l, s, T, segs, ctx):
        tp, tb = self.tp, self.tp_b
        SB = self.slab_b
        V = lambda c: self.vec[:, l, c + s:c + s + 1]
        nch = T // CH
        npair = T // 128
        bon = ctx["bon"]
        par = ctx["par"]
        rhat, rhb = self.rhat2[par], self.rhat2_b[par]
        gam, gamb = self.gam2[par], self.gam2_b[par]
        y_f = self.talloc()
        its = [(pr, hh) for pr in range(npair) for hh in range(2)]
        G = len(its)
        GW = min(G, LOCK_W)
        hps = [slice(hh * 64, (hh + 1) * 64) for (_, hh) in its]
        colss = [slice(pr * 128, (pr + 1) * 128) for (pr, _) in its]
        for g0 in range(0, G, GW):
          grp = list(range(g0, min(G, g0 + GW)))
          for i, (pr, hh) in [(ii, its[ii]) for ii in grp]:
              hp, cols = hps[i], colss[i]
              Nm, Nmb = self.NmG[i], self.NmG_b[i]
              ps1, pb1 = self.bank()
              rqv = self.rq[hp, :, cols]
              self.MM(ps1[:, 0:256], self.bt[hp, cols], rqv, [SB["bt"], SB["rq"]], [pb1], signal=False)
              self.MM(ps1[:, 256:512], self.kt[hp, cols], rqv, [SB["kt"], SB["rq"]], [pb1])
              self.Vtt(Nm[:, :], ps1[:, :], self.mask4, ALU.mult, [pb1, self.cst_b], [Nmb])
              yield
          for i, (pr, hh) in [(ii, its[ii]) for ii in grp]:
              hp, cols = hps[i], colss[i]
              XX, XXb = self.XXG[i], self.XXG_b[i]
              tm, tmb = self.tm[pr], self.tm_b[pr]
              ps2, pb2 = self.bank()
              self.MM(ps2[:, 0:128], self.rq[hp, 0, cols], self.bt[hp, cols], [SB["bt"], SB["rq"]], [pb2], signal=False)
              self.MM(ps2[:, 128:192], self.NmG[i][:, 256:384], tm[:, 3, hps[i]], [self.NmG_b[i], tmb], [pb2])
              self.Vtt(XX[:, 0, 1, :], ps2[:, 0:128], self.mls, ALU.mult, [pb2, self.cst_b], [XXb[0]])
              Zb, Zbb = self.ZbG[i], self.ZbG_b[i]
              self.A(Zb[:, 0, 0:64], ps2[:, 128:192], AF.Copy, [pb2], [Zbb[0]])
              self.Vcp(Zb[:, 0, 64:128], tm[:, 0, hps[i]], [tmb], [Zbb[0]])
              yield
          if g0 + GW >= G:
              ctx["A_done"] = True
          for j in range(6):
              banks = []
              for i in grp:
                  Nm, Nmb = self.NmG[i], self.NmG_b[i]
                  XX, XXb = self.XXG[i], self.XXG_b[i]
                  Zb, Zbb = self.ZbG[i], self.ZbG_b[i]
                  sl = j % 2
                  if j == 0:
                      Xj, XTj, xr = Nm[:, 0:128], XX[:, 0, 1, :], [Nmb, XXb[0]]
                  else:
                      Xj, XTj, xr = XX[:, sl, 0, :], XX[:, sl, 1, :], [XXb[sl]]
                  psq, pbq = self.bank()
                  banks.append((psq, pbq))
                  if j < 5:
                      self.MM(psq[:, 0:128], XTj, Xj, xr, [pbq], signal=False)
                      self.MM(psq[:, 128:256], Xj, XTj, xr, [pbq], signal=False)
                  self.MM(psq[:, 256:384], Xj, Zb[:, sl, :], xr + [Zbb[sl]], [pbq])
                  if i % 2 == 1:
                      yield
              for i in grp:
                  XX, XXb = self.XXG[i], self.XXG_b[i]
                  Zb, Zbb = self.ZbG[i], self.ZbG_b[i]
                  sl = j % 2
                  psq, pbq = banks[i - g0]
                  self.Vtt(Zb[:, 1 - sl, :], Zb[:, sl, :], psq[:, 256:384], ALU.add, [Zbb[sl], pbq], [Zbb[1 - sl]])
                  if j < 5:
                      self.A(XX[:, 1 - sl, :, :].rearrange("p a b -> p (a b)"), psq[:, 0:256], AF.Copy, [pbq], [XXb[1 - sl]])
                  if i % 2 == 1:
                      yield
          for i, (pr, hh) in [(ii, its[ii]) for ii in grp]:
              hp, cols = hps[i], colss[i]
              hc = hp
              Nm, Nmb = self.NmG[i], self.NmG_b[i]
              tm, tmb = self.tm[pr], self.tm_b[pr]
              Zf, Zfb = self.ZbG[i][:, 0, :], self.ZbG_b[i][0]
              psr, pbr = self.bank()
              self.MM(psr[hp, 0:128], Zf[:, 64:128], Nm[:, 128:256], [Zfb, Nmb], [pbr], signal=False)
              self.MM(psr[hp, 128:256], Zf[:, 0:64], Nm[:, 128:256], [Zfb, Nmb], [pbr], start=True, stop=False, signal=False)
              self.MM(psr[hp, 128:256], tm[:, 3, hc], Nm[:, 384:512], [tmb, Nmb], [pbr], start=False, stop=True)
              self.Vtt(rhat[hp, cols], rhat[hp, cols], psr[hp, 0:128], ALU.add, [rhb, pbr], [rhb])
              self.A(tp[y_f][hp, cols], psr[hp, 128:256], AF.Copy, [pbr], [tb[y_f]])
              yield
          for i, (pr, hh) in [(ii, its[ii]) for ii in grp]:
              hp, cols = hps[i], colss[i]
              hc = hp
              tm, tmb = self.tm[pr], self.tm_b[pr]
              tmm, tmmb = self.tmm[pr], self.tmm_b[pr]
              Zf, Zfb = self.ZbG[i][:, 0, :], self.ZbG_b[i][0]
              psp, pbp = self.bank()
              self.MM(psp[hp, 0:128], Zf[:, 64:128], tmm[:, 0, :, hc], [Zfb, tmmb], [pbp], signal=False)
              for c in range(2):
                  o = slice(128 + c * 64, 128 + (c + 1) * 64)
                  self.MM(psp[hp, o], tmm[:, 0, c, hc], Zf[:, 0:64], [Zfb, tmmb], [pbp], start=True, stop=False, signal=False)
                  self.MM(psp[hp, o], tmm[:, 1, c, hc], tm[:, 3, hc], [tmb, tmmb], [pbp], start=False, stop=True, signal=(c == 1))
              for c in range(2):
                  chn = pr * 2 + c
                  self.A(self.PTbd[hp, chn, hc], psp[hp, c * 64:(c + 1) * 64], AF.Copy, [pbp], [self.PT_b[chn]])
                  self.Vcp(self.Gs[hp, chn, :], psp[hp, 128 + c * 64:128 + (c + 1) * 64], [pbp], [self.G_b[chn]])
              yield
        ctx["tm_done"] = True
        H = None
        bdv = self.bdmask.rearrange("p (a v) -> p a v", a=2)
        Hbd3 = self.Hbd[:].rearrange("p (a v) -> p a v", a=2)
        for c in range(nch if "chain" not in self.cfg.skip else 0):
            t0 = c * CH
            for (c0, n, info) in segs:
                if c0 <= t0 < c0 + n:
                    break
            if t0 == c0:
                if info["sample"]:
                    b = info["b"]
                    H, Hbuf = self.Hs[:, b, :], self.Hs_b[b]
                    self.dma(self.sp, self.Hs_ch[b], H, self.H0_d[l, b, s], writes=[Hbuf])
                else:
                    H, Hbuf = self.Hst[:, l, s, :], self.H_b[l][s]
                self.A(self.Hb[:, :], H, AF.Copy, [Hbuf], [self.Hb_b])
                self.Vtt(Hbd3, H.unsqueeze(1).broadcast_to([128, 2, 64]), bdv, ALU.mult, [Hbuf, self.cst_b], [self.Hbd_b])
            cc = slice(t0, t0 + CH)
            self.Vstt(self.Hn[:, :], H, gam[:, c:c + 1], self.Gs[:, c, :], ALU.mult, ALU.add, [Hbuf, gamb, self.G_b[c]], [self.Hn_b])
            psh, pbh = self.bank()
            self.MM(psh[:, 0:64], self.PTbd[:, c, :], self.Hb[:, :], [self.PT_b[c], self.Hb_b], [pbh])
            psy, pby = self.bank()
            self.MM(psy[:, 0:64], self.Hbd[:, :], rhat[:, cc], [self.Hbd_b, rhb], [pby])
            yield
            last_of_seg = (t0 + CH == c0 + n)
            if not last_of_seg:
                self.Vtt(self.Hb[:, :], self.Hn[:, :], psh[:, 0:64], ALU.add, [self.Hn_b, pbh], [self.Hb_b])
            self.Vtt(H, self.Hn[:, :], psh[:, 0:64], ALU.add, [self.Hn_b, pbh], [Hbuf])
            if not last_of_seg:
                self.Vtt(Hbd3, H.unsqueeze(1).broadcast_to([128, 2, 64]), bdv, ALU.mult, [Hbuf, self.cst_b], [self.Hbd_b])
            self.Vtt(tp[y_f][:, cc], tp[y_f][:, cc], psy[:, 0:64], ALU.add, [tb[y_f], pby], [tb[y_f]])
            if last_of_seg and info["last"]:
                och = self.wkv_ch[info["seq"]]
                self.dma(self.sp, och, self.wkvo_d[l, info["seq"], s], H, reads=[Hbuf])
            yield
        psm, pbm = self.bank()
        self.MM(psm[:, :T], self.bo1, tp[y_f][:, :T], [tb[y_f], self.cst_b], [pbm])
        yc = self.talloc()
        self.Vstt(tp[yc][:, :T], psm[:, :T], -1.0 / 64, tp[y_f][:, :T], ALU.mult, ALU.add, [pbm, tb[y_f]], [tb[yc]])
        self.A(tp[y_f][:, :T], tp[yc][:, :T], AF.Square, [tb[yc]], [tb[y_f]])
        yield
        psv, pbv = self.bank()
        self.MM(psv[:, :T], self.bo1, tp[y_f][:, :T], [tb[y_f], self.cst_b], [pbv])
        self.A(tp[y_f][:, :T], psv[:, :T], AF.Sqrt, [pbv, self.misc_b], [tb[y_f]], bias=self.eps_gn[:], scale=1.0 / 64)
        yield
        self.op(self.dve, lambda e: e.reciprocal(out=tp[y_f][:, :T], in_=tp[y_f][:, :T]), reads=[tb[y_f]], writes=[tb[y_f]])
        self.Vtt(tp[yc][:, :T], tp[yc][:, :T], tp[y_f][:, :T], ALU.mult, [tb[yc], tb[y_f]], [tb[yc]])
        yield
        self.Vts(tp[yc][:, :T], tp[yc][:, :T], V(VC_LW), ALU.mult, [tb[yc], self.vec_b], [tb[yc]], s2=V(VC_LB), op1=ALU.add)
        self.Vtt(tp[yc][:, :T], tp[yc][:, :T], tp[bon][:, :T], ALU.add, [tb[yc], tb[bon]], [tb[yc]])
        yield
        ps_g, pb_g = self.bank()
        self.MM(ps_g[:, :T], self.lw[:, 1024 + s * 128:1024 + (s + 1) * 128], self.sgb[:, :T], [self.lw_b, self.sgb_b], [pb_g])
        self.Vtt(self.yrT[:, s, :T], tp[yc][:, :T], ps_g[:, :T], ALU.mult, [tb[yc], pb_g], [self.yr_b[s]])
        self.tfree(yc, y_f, bon)

    def final_out(self, tok0, T):
        gf = lambda kc: self.nf[:, kc:kc + 1]
        i = self.rstd_of_x(T)
        dst = self.yT_d.rearrange("(kc p) t -> p kc t", p=128)
        for kc in range(KC):
            self.Vstt(self.xT[:, kc, :T], self.xT[:, kc, :T], gf(kc), self.tp[i][:, :T], ALU.mult, ALU.mult,
                      [self.xT_b[kc], self.tp_b[i], self.nf_b], [self.xT_b[kc]])
        self.tfree(i)
        self.dma(self.sp, self.yst_ch, dst[:, :, tok0:tok0 + T], self.xT[:, :, :T], reads=self.xT_b)

    def finish(self):
        for ch in self.out_chans:
            if ch.count:
                self.sp.h.wait_ge(ch.sem, ch.count)

    def build(self):
        cfg = self.cfg
        self.alloc()
        self.setup()
        tiles = []
        for i in range(cfg.npt):
            tiles.append((i * TT, TT, [(0, TT, dict(sample=False, first=(i == 0), last=(i == cfg.npt - 1), seq=0, b=0))]))
        if cfg.sample:
            tiles.append((cfg.npt * TT, 128, [(0, 64, dict(sample=True, first=True, last=True, seq=1, b=0)),
                                             (64, 64, dict(sample=True, first=True, last=True, seq=2, b=1))]))
        for ti, (tok0, T, segs) in enumerate(tiles):
            self.load_x(tok0, T)
            for l in range(cfg.depth):
                if cfg.do_mix:
                    self.mixer(l, T, segs, ti)
                if cfg.do_mlp:
                    self.mlp(l, T)
            self.final_out(tok0, T)
        self.finish()
        return self.nc


C_ID, C_ONES, C_BO, C_M4, C_MLS, C_CHM, C_ROWM, C_BDM = 0, 128, 256, 384, 896, 1024, 1536, 1538
NCST = 1666


def host_consts():
    c = np.zeros((128, NCST), np.float32)
    p = np.arange(128)
    c[:, C_ID:C_ID + 128] = np.eye(128, dtype=np.float32)
    c[:, C_ONES:C_ONES + 128] = 1.0
    same = (p[:, None] // 64) == (p[None, :] // 64)
    c[:, C_BO:C_BO + 128] = same
    mus = same & (p[:, None] < p[None, :])
    mui = same & (p[:, None] <= p[None, :])
    c[:, C_M4:C_M4 + 512] = np.concatenate([mus, mui, mus, mui], 1)
    c[:, C_MLS:C_MLS + 128] = same & (p[:, None] > p[None, :])
    cm = np.ones(512, np.float32)
    cm[::64] = 0.0
    c[:, C_CHM:C_CHM + 512] = cm[None, :]
    c[:, C_ROWM + 0] = (p < 64)
    c[:, C_ROWM + 1] = (p >= 64)
    bd = np.zeros((128, 2, 64), np.float32)
    bd[:64, 0, :] = 1.0
    bd[64:, 1, :] = 1.0
    c[:, C_BDM:C_BDM + 128] = bd.reshape(128, 128)
    return c


def host_vec(inp, L):
    v = np.zeros((L, 128, NVC), np.float32)
    fm = lambda a, n: np.asarray(a, np.float32).reshape(n, 128).T
    for l in range(L):
        v[l, :, VC_N1:VC_N1 + 16] = fm(inp["norm1"][l], 16)
        v[l, :, VC_N2:VC_N2 + 16] = fm(inp["norm2"][l], 16)
        v[l, :, VC_MU:VC_MU + 26] = fm(inp["mu_shift"][l], 26)
        for j in range(3):
            v[l, :, VC_CW + j * 8:VC_CW + j * 8 + 8] = fm(inp["conv_w"][l, j], 8)
        v[l, :, VC_W0:VC_W0 + 8] = fm(inp["w0"][l], 8)
        v[l, :, VC_A0:VC_A0 + 8] = fm(inp["a0"][l], 8)
        v[l, :, VC_KK:VC_KK + 8] = fm(inp["k_k"][l], 8)
        v[l, :, VC_KA:VC_KA + 8] = fm(inp["k_a"][l], 8)
        v[l, :, VC_RK:VC_RK + 8] = fm(np.asarray(inp["r_k"][l]).reshape(-1), 8)
        v[l, :, VC_LW:VC_LW + 8] = fm(inp["ln_x_w"][l], 8)
        v[l, :, VC_LB:VC_LB + 8] = fm(inp["ln_x_b"][l], 8)
    return v


def make_in_map(inp, cfg, xp, xs, cconv, sshift, swkv):
    L = cfg.depth
    m = {"vec": host_vec(inp, L), "nf": np.ascontiguousarray(np.asarray(inp["norm_f"], np.float32).reshape(16, 128).T),
         "cst": host_consts()}
    for k in ("w_in", "w2", "a2", "g2", "w_out_conv", "w_out_rwkv", "w_o", "w_up", "w_down"):
        m[k] = np.ascontiguousarray(np.asarray(inp[k], np.float32)[:L])
    m.update(make_in_map_states(cfg, xp, xs, cconv, sshift, swkv))
    return m


def make_in_map_states(cfg, xp, xs, cconv, sshift, swkv):
    L = cfg.depth
    m = {}
    toks = [np.asarray(xp, np.float32)]
    if cfg.sample:
        toks.append(np.asarray(xs, np.float32).reshape(128, D))
    m["xT"] = np.ascontiguousarray(np.concatenate(toks, 0).T)
    hs0 = np.zeros((L, 128, 16, 2), np.float32)
    cv0 = np.zeros((L, 128, 8, 2, 2), np.float32)
    H0 = np.zeros((L, 2, 8, 128, 64), np.float32)
    if cfg.sample:
        ss = np.asarray(sshift, np.float32)[:L]
        hs0[:] = ss.reshape(L, 2, 16, 128).transpose(0, 3, 2, 1)
        cc = np.asarray(cconv, np.float32)[:L]
        cv0[:] = cc.reshape(L, 2, 2, 8, 128).transpose(0, 4, 3, 1, 2)
        sw = np.asarray(swkv, np.float32)[:L]
        H0[:] = sw.reshape(L, 2, 8, 2, 64, 64).transpose(0, 1, 2, 3, 5, 4).reshape(L, 2, 8, 128, 64)
    m["hs0"], m["cv0"], m["H0"] = hs0, cv0, H0
    return m


_NC_CACHE = {}


def kernel(**inp):
    cfg = Cfg(npt=8, depth=4, sample=True)
    n = 8
    xprompt = np.asarray(inp["x_prompt"], np.float32)
    xsample = np.asarray(inp["x_sample"], np.float32)
    cconv = np.asarray(inp["cache_conv"], np.float32)
    sshift = np.asarray(inp["state_shift"], np.float32)
    swkv = np.asarray(inp["state_wkv"], np.float32)
    kb = KB(cfg)
    nc = kb.build()
    base = make_in_map(inp, cfg, xprompt[0], xsample[0:2], cconv[:, 0:2], sshift[:, 0:2], swkv[:, 0:2])
    in_maps = []
    for c in range(n):
        m = dict(base)
        if c > 0:
            mc = make_in_map_states(cfg, xprompt[c % 4], xsample[2 * c:2 * c + 2], cconv[:, 2 * c:2 * c + 2], sshift[:, 2 * c:2 * c + 2], swkv[:, 2 * c:2 * c + 2])
            m.update(mc)
        in_maps.append(m)
    res = run_bass_kernel_spmd(nc, in_maps, core_ids=list(range(n)))
    L = cfg.depth
    y_p = np.zeros((4, 4096, D), np.float32)
    y_s = np.zeros((16, 64, D), np.float32)
    conv_p = np.zeros((L, 4, 2, 1024), np.float32)
    shift_p = np.zeros((L, 4, D), np.float32)
    wkv_p = np.zeros((L, 4, 16, 64, 64), np.float32)
    conv_s = np.zeros((L, 16, 2, 1024), np.float32)
    shift_s = np.zeros((L, 16, D), np.float32)
    wkv_s = np.zeros((L, 16, 16, 64, 64), np.float32)
    fc = lambda a: a.transpose(0, 3, 2, 1).reshape(L, 2, 1024)
    fs = lambda a: a.transpose(0, 2, 1).reshape(L, D)
    fw = lambda a: a.reshape(L, 8, 2, 64, 64).transpose(0, 1, 2, 4, 3).reshape(L, 16, 64, 64)
    for c in range(n):
        r = res.results[c]
        yT = np.asarray(r["yT"])
        convo, shifto, wkvo = np.asarray(r["convo"]), np.asarray(r["shifto"]), np.asarray(r["wkvo"])
        if c < 4:
            y_p[c] = yT[:, :4096].T
            conv_p[:, c] = fc(convo[:, 0])
            shift_p[:, c] = fs(shifto[:, 0])
            wkv_p[:, c] = fw(wkvo[:, 0])
        y_s[2 * c:2 * c + 2] = yT[:, 4096:].T.reshape(2, 64, D)
        for j in range(2):
            conv_s[:, 2 * c + j] = fc(convo[:, 1 + j])
            shift_s[:, 2 * c + j] = fs(shifto[:, 1 + j])
            wkv_s[:, 2 * c + j] = fw(wkvo[:, 1 + j])
    return (y_p, y_s, conv_p, shift_p, wkv_p, conv_s, shift_s, wkv_s)
```

```python
import contextlib
import math
import numpy as np
import concourse.bass as bass
import concourse.mybir as mybir
from concourse.bass_utils import run_bass_kernel_spmd

F32 = mybir.dt.float32
BF16 = mybir.dt.bfloat16
AF = mybir.ActivationFunctionType
ALU = mybir.AluOpType

D = 2048
KC = 16
DCONV = 1024
DRW = 1024
NH = 16
HD = 64
DIN = 10496
DFF = 8192
DEPTH_FULL = 4
RMS_EPS = 1e-6
GN_EPS = 64e-5
DS = math.exp(-0.5)
OFF_R = 3072
OFF_LORA = 6144
OFF_GC = 6400
OFF_GR = 8448
TT = 512
SAME_ENGINE_NOSYNC = False
NODMA = False
GEN_REPS = (1, 1)
LOCK_W = 8
CH = 64
VC_N1 = 0
VC_N2 = 16
VC_MU = 32
VC_CW = 58
VC_W0 = 82
VC_A0 = 90
VC_KK = 98
VC_KA = 106
VC_RK = 114
VC_LW = 122
VC_LB = 130
NVC = 138


class Eng:
    def __init__(self, name, h, sem, is_pe=False):
        self.name, self.h, self.sem, self.is_pe = name, h, sem, is_pe
        self.count = 0
        self.waited = {}


class Chan:
    def __init__(self, name, sem):
        self.name, self.sem = name, sem
        self.count = 0


class Buf:
    __slots__ = ("w", "r", "name", "excl")

    def __init__(self, name="", excl=False):
        self.w = None
        self.r = {}
        self.name = name
        self.excl = excl


class Cfg:
    def __init__(self, npt=8, depth=4, sample=True, do_mix=True, do_mlp=True, do_rwkv=True):
        self.npt, self.depth, self.sample = npt, depth, sample
        self.do_mix, self.do_mlp, self.do_rwkv = do_mix, do_mlp, do_rwkv
        self.dbg = 0
        self.skip = set()
        self.ntok = npt * TT + (128 if sample else 0)


class KB:
    def __init__(self, cfg):
        self.cfg = cfg
        self.nc = nc = bass.Bass("TRN2", target_bir_lowering=False)
        self.es = contextlib.ExitStack()
        self.nsem = 0
        L = cfg.depth
        NT = cfg.ntok
        di = lambda n, s: nc.dram_tensor(n, list(s), F32, kind="ExternalInput").ap()
        do = lambda n, s: nc.dram_tensor(n, list(s), F32, kind="ExternalOutput").ap()
        self.xT_d = di("xT", (D, NT))
        self.vec_d = di("vec", (L, 128, NVC))
        self.nf_d = di("nf", (128, 16))
        self.w_in_d = di("w_in", (L, D, DIN))
        self.w2_d = di("w2", (L, 64, DRW))
        self.a2_d = di("a2", (L, 64, DRW))
        self.g2_d = di("g2", (L, 128, DRW))
        self.woc_d = di("w_out_conv", (L, DCONV, D))
        self.wor_d = di("w_out_rwkv", (L, DRW, D))
        self.wo_d = di("w_o", (L, D, D))
        self.wup_d = di("w_up", (L, D, DFF))
        self.wdn_d = di("w_down", (L, DFF, D))
        self.cst_d = di("cst", (128, NCST))
        self.hs0_d = di("hs0", (L, 128, 16, 2))
        self.cv0_d = di("cv0", (L, 128, 8, 2, 2))
        self.H0_d = di("H0", (L, 2, 8, 128, 64))
        self.yT_d = do("yT", (D, NT))
        self.convo_d = do("convo", (L, 3, 128, 8, 2))
        self.shifto_d = do("shifto", (L, 3, 128, 16))
        self.wkvo_d = do("wkvo", (L, 3, 8, 128, 64))

        ds = lambda n, s: nc.dram_tensor(n, list(s), BF16, kind="Internal").ap()
        self.w_in_s = ds("w_in_s", (L, D, DIN))
        self.w2_s = ds("w2_s", (L, 64, DRW))
        self.a2_s = ds("a2_s", (L, 64, DRW))
        self.g2_s = ds("g2_s", (L, 128, DRW))
        self.woc_s = ds("woc_s", (L, DCONV, D))
        self.wor_s = ds("wor_s", (L, DRW, D))
        self.wo_s = ds("wo_s", (L, D, D))
        self.wup_s = ds("wup_s", (L, D, DFF))
        self.wdn_s = ds("wdn_s", (L, DFF, D))
        self.cvb = {}
        sem = self.new_sem
        self.pe = Eng("pe", nc.tensor, sem("pe"), is_pe=True)
        self.act = Eng("act", nc.scalar, sem("act"))
        self.dve = Eng("dve", nc.vector, sem("dve"))
        self.pool = Eng("pool", nc.gpsimd, sem("pool"))
        self.sp = Eng("sp", nc.sync, sem("sp"))
        self.out_chans = []

    def new_sem(self, name):
        self.nsem += 1
        return self.es.enter_context(self.nc.semaphore(f"s{self.nsem}_{name}"))

    def chan(self, name):
        return Chan(name, self.new_sem("c_" + name))

    def sb(self, name, shape, dt):
        return self.es.enter_context(self.nc.sbuf_tensor(name, list(shape), dt))

    def _deps(self, eng, reads, writes):
        deps = {}

        def need(st):
            if st is None:
                return
            o, v = st
            if deps.get(o, 0) < v:
                deps[o] = v
        for b in reads:
            need(b.w)
            if b.excl:
                for o, v in b.r.items():
                    if o is not eng:
                        need((o, v))
        for b in writes:
            need(b.w)
            for o, v in b.r.items():
                need((o, v))
        for o, v in deps.items():
            if o is eng and (eng.is_pe or SAME_ENGINE_NOSYNC):
                continue
            if eng.waited.get(o, 0) >= v:
                continue
            assert v <= o.count, (eng.name, o.name, v, o.count)
            eng.h.wait_ge(o.sem, v)
            eng.waited[o] = v

    def _stamp(self, st, reads, writes):
        o, v = st
        for b in reads:
            if b.r.get(o, 0) < v:
                b.r[o] = v
        for b in writes:
            b.w = st
            b.r = {}

    def op(self, eng, fn, reads=(), writes=(), signal=True):
        self._deps(eng, reads, writes)
        ins = fn(eng.h)
        if signal:
            eng.count += 1
            ins.then_inc(eng.sem, 1)
            st = (eng, eng.count)
        else:
            st = (eng, eng.count + 1)
        self._stamp(st, reads, writes)
        return ins

    def dma(self, eng, ch, out, in_, reads=(), writes=(), **kw):
        self._deps(eng, reads, writes)
        ins = eng.h.dma_start(out=out, in_=in_, **kw)
        ch.count += 16
        ins.then_inc(ch.sem, 16)
        self._stamp((ch, ch.count), reads, writes)
        return ins

    def bank(self):
        i = self.bank_i
        while i in self.reserved:
            i = (i + 1) % 8
        self.bank_i = (i + 1) % 8
        return self.ps[i], self.psb[i]

    def alloc(self):
        nc = self.nc
        sb = self.sb
        L = self.cfg.depth
        self.xT = sb("xTs", (128, KC, TT), F32)
        self.xT_b = [Buf(f"xT{k}") for k in range(KC)]
        self.hT = sb("hTs", (128, KC, TT + 4), BF16)
        self.hT_b = [Buf(f"hT{k}") for k in range(KC)]
        self.hTx_b = Buf("hTx")
        self.upT = sb("upTs", (128, KC, TT), BF16)
        self.upT_b = [Buf(f"upT{k}") for k in range(KC)]
        self.wt = [sb(f"wt{i}", (128, 8192), BF16) for i in range(2)]
        self.wt_b = [Buf(f"wt{i}") for i in range(2)]
        self.wt_ch = [self.chan(f"wt{i}") for i in range(2)]
        self.wt_i = 0
        self.cst = sb("csts", (128, NCST), F32)
        self.cst_b = Buf("cst")
        self.vec = sb("vecs", (128, L, NVC), F32)
        self.vec_b = Buf("vec")
        self.nf = sb("nfs", (128, 16), F32)
        self.identb = sb("identb", (128, 128), BF16)
        self.eps_rms = sb("eps_rms", (128, 1), F32)
        self.eps_gn = sb("eps_gn", (128, 1), F32)
        self.misc_b = Buf("misc")
        self.ps = [self.es.enter_context(nc.psum_tensor(f"ps{i}", [128, 512], F32)) for i in range(8)]
        self.psb = [Buf(f"ps{i}", excl=True) for i in range(8)]
        self.bank_i = 0
        self.reserved = set()
        self.norm_state = None
        self.ld_ch = self.chan("ld")
        self.xld_ch = self.chan("xld")
        self.yst_ch = self.chan("yst")
        self.out_chans.append(self.yst_ch)
        NP = 14
        self.tp = [sb(f"tp{i}", (128, TT + 4), F32) for i in range(NP)]
        self.tp_b = [Buf(f"tp{i}") for i in range(NP)]
        self.tp_free = list(range(NP))
        self.ycinT = sb("ycinT", (128, 8, TT), BF16)
        self.ycin_b = [Buf(f"ycin{k}") for k in range(8)]
        self.yrT = sb("yrT", (128, 8, TT), BF16)
        self.yr_b = [Buf(f"yr{k}") for k in range(8)]
        self.lw = sb("lw", (128, 2048), BF16)
        self.lw_b = Buf("lw")
        self.lw_ch = self.chan("lw")
        self.lwin = sb("lwin", (128, TT + 2), BF16)
        self.lwin_b = Buf("lwin")
        self.sgb = sb("sgb", (128, TT), BF16)
        self.sgb_b = Buf("sgb")
        self.plast = sb("plast", (128, L, 26), F32)
        self.plast_b = Buf("plast")
        self.ulast = sb("ulast", (128, L, 8, 2), F32)
        self.ulast_b = Buf("ulast")
        self.hs0 = sb("hs0s", (128, L, 16, 2), F32)
        self.cv0 = sb("cv0s", (128, L, 8, 2, 2), F32)
        self.st_b = Buf("states")
        self.rq = sb("rq", (128, 2, TT), BF16)
        self.kt = sb("kt", (128, TT), BF16)
        self.bt = sb("bt", (128, TT), BF16)
        self.ktp = sb("ktp", (128, TT), BF16)
        self.btp = sb("btp", (128, TT), BF16)
        self.vb = sb("vb", (128, TT), BF16)
        self.rhat2 = [sb(f"rhat{i}", (128, TT), BF16) for i in range(2)]
        self.rhat2_b = [Buf(f"rhat{i}") for i in range(2)]
        self.gam2 = [sb(f"gam{i}", (128, 8), F32) for i in range(2)]
        self.gam2_b = [Buf(f"gam{i}") for i in range(2)]
        self.slab_b = {n: Buf(n) for n in ("rq", "kt", "bt", "ktp", "btp", "vb")}
        self.tm = [sb(f"tm{i}", (128, 4, 128), BF16) for i in range(4)]
        self.tm_b = [Buf(f"tm{i}") for i in range(4)]
        self.tmm = [sb(f"tmm{i}", (128, 2, 2, 128), BF16) for i in range(4)]
        self.tmm_b = [Buf(f"tmm{i}") for i in range(4)]
        self.NmG = [sb(f"NmG{i}", (128, 512), BF16) for i in range(8)]
        self.NmG_b = [Buf(f"NmG{i}") for i in range(8)]
        self.XXG = [sb(f"XXG{i}", (128, 2, 2, 128), BF16) for i in range(8)]
        self.XXG_b = [[Buf(f"XXG{i}_{j}") for j in range(2)] for i in range(8)]
        self.ZbG = [sb(f"ZbG{i}", (128, 2, 128), BF16) for i in range(8)]
        self.ZbG_b = [[Buf(f"ZbG{i}_{j}") for j in range(2)] for i in range(8)]
        self.set_i = 0
        self.PTbd = sb("PTbd", (128, 8, 128), BF16)
        self.PT_b = [Buf(f"PT{c}") for c in range(8)]
        self.Gs = sb("Gs", (128, 8, 64), F32)
        self.G_b = [Buf(f"G{c}") for c in range(8)]
        self.Hst = sb("Hst", (128, L, 8, 64), F32)
        self.H_b = [[Buf(f"H{l}_{s}") for s in range(8)] for l in range(L)]
        self.Hs = sb("Hs", (128, 2, 64), F32)
        self.Hs_b = [Buf("Hs0"), Buf("Hs1")]
        self.Hs_ch = [self.chan("Hs0"), self.chan("Hs1")]
        self.Hb = sb("Hb", (128, 64), BF16)
        self.Hb_b = Buf("Hb")
        self.Hbd = sb("Hbd", (128, 128), BF16)
        self.Hbd_b = Buf("Hbd")
        self.Hn = sb("Hn", (128, 64), F32)
        self.Hn_b = Buf("Hn")
        self.ostage = sb("ostage", (128, 48), F32)
        self.ost_b = Buf("ostage")
        self.ost_ch = self.chan("ost")
        self.out_chans.append(self.ost_ch)
        self.wkv_ch = [self.chan(f"wkvo{i}") for i in range(3)]
        self.out_chans.extend(self.wkv_ch)

    def talloc(self):
        i = self.tp_free.pop(0)
        return i

    def tfree(self, *idx):
        for i in idx:
            assert i not in self.tp_free
            self.tp_free.append(i)

    def A(self, out, in_, func, r, w, bias=None, scale=None):
        kw = {}
        if bias is not None:
            kw["bias"] = bias
        if scale is not None:
            kw["scale"] = scale
        return self.op(self.act, lambda e: e.activation(out=out, in_=in_, func=func, **kw), reads=r, writes=w)

    def Vtt(self, out, in0, in1, op, r, w):
        return self.op(self.dve, lambda e: e.tensor_tensor(out=out, in0=in0, in1=in1, op=op), reads=r, writes=w)

    def Vts(self, out, in0, s1, op0, r, w, s2=None, op1=None):
        if op1 is None:
            return self.op(self.dve, lambda e: e.tensor_scalar(out=out, in0=in0, scalar1=s1, scalar2=None, op0=op0), reads=r, writes=w)
        return self.op(self.dve, lambda e: e.tensor_scalar(out=out, in0=in0, scalar1=s1, scalar2=s2, op0=op0, op1=op1), reads=r, writes=w)

    def Vstt(self, out, in0, scalar, in1, op0, op1, r, w):
        return self.op(self.dve, lambda e: e.scalar_tensor_tensor(out=out, in0=in0, scalar=scalar, in1=in1, op0=op0, op1=op1), reads=r, writes=w)

    def Vcp(self, out, in_, r, w):
        return self.op(self.dve, lambda e: e.tensor_copy(out=out, in_=in_), reads=r, writes=w)

    def MM(self, out, lhsT, rhs, r, w, start=True, stop=True, signal=True):
        return self.op(self.pe, lambda e: e.matmul(out, lhsT, rhs, start=start, stop=stop), reads=r, writes=w, signal=signal)

    def TR(self, out, in_, r, w, signal=True):
        return self.op(self.pe, lambda e: e.transpose(out, in_, self.identb[:]), reads=r + [self.misc_b], writes=w, signal=signal)

    def cslice(self, c0, n):
        return self.cst[:, c0:c0 + n]

    def setup(self):
        L = self.cfg.depth
        self.nf_b = Buf("nf")
        self.st2_b = Buf("st2")
        self.dma(self.sp, self.chan("ld1"), self.cst[:], self.cst_d[:, :], writes=[self.cst_b])
        self.dma(self.sp, self.chan("ld2"), self.vec[:], self.vec_d.rearrange("l p c -> p l c"), writes=[self.vec_b])
        self.dma(self.sp, self.chan("ld3"), self.nf[:], self.nf_d[:, :], writes=[self.nf_b])
        self.dma(self.sp, self.chan("ld4"), self.hs0[:], self.hs0_d.rearrange("l p k b -> p l k b"), writes=[self.st_b])
        self.dma(self.sp, self.chan("ld5"), self.cv0[:], self.cv0_d.rearrange("l p c b j -> p l c b j"), writes=[self.st2_b])
        self.op(self.dve, lambda e: e.memset(self.eps_rms[:], RMS_EPS), writes=[self.misc_b])
        self.op(self.dve, lambda e: e.memset(self.eps_gn[:], GN_EPS), writes=[self.misc_b])
        self.ident_f = self.cslice(C_ID, 128)
        self.ones_f = self.cslice(C_ONES, 128)
        self.bo1 = self.cslice(C_BO, 128)
        self.mask4 = self.cslice(C_M4, 512)
        self.mls = self.cslice(C_MLS, 128)
        self.chmask = self.cslice(C_CHM, 512)
        self.rowmask = self.cslice(C_ROWM, 2)
        self.bdmask = self.cslice(C_BDM, 128)
        self.Vcp(self.identb[:], self.ident_f, [self.cst_b], [self.misc_b])
        self.op(self.dve, lambda e: e.memset(self.plast[:], 0.0), writes=[self.plast_b])
        self.op(self.dve, lambda e: e.memset(self.ulast[:], 0.0), writes=[self.ulast_b])
        self.op(self.dve, lambda e: e.memset(self.PTbd[:], 0.0), writes=self.PT_b)
        for l in range(L):
            self.op(self.dve, lambda e: e.memset(self.Hst[:, l], 0.0), writes=self.H_b[l])
        def cv(key, dst, src_):
            b = Buf("cv_" + str(key))
            self.cvb[key] = b
            self.dma(self.pool, self.chan("cv"), dst, src_, writes=[b])
        for l in range(L):
            cv(("win_c", l), self.w_in_s[l][:, 0:3072], self.w_in_d[l][:, 0:3072])
            cv(("win_r", l), self.w_in_s[l][:, 3072:6400], self.w_in_d[l][:, 3072:6400])
            cv(("w2", l), self.w2_s[l], self.w2_d[l])
            cv(("a2", l), self.a2_s[l], self.a2_d[l])
            cv(("g2", l), self.g2_s[l], self.g2_d[l])
            cv(("win_g", l), self.w_in_s[l][:, 6400:DIN], self.w_in_d[l][:, 6400:DIN])
            cv(("woc", l), self.woc_s[l], self.woc_d[l])
            cv(("wor", l), self.wor_s[l], self.wor_d[l])
            cv(("wo", l), self.wo_s[l], self.wo_d[l])
            for g in range(4):
                cv(("wup", l, g), self.wup_s[l][:, g * 2048:(g + 1) * 2048], self.wup_d[l][:, g * 2048:(g + 1) * 2048])
            for g in range(4):
                cv(("wdn", l, g), self.wdn_s[l][g * 2048:(g + 1) * 2048, :], self.wdn_d[l][g * 2048:(g + 1) * 2048, :])

    def load_x(self, tok0, T):
        src = self.xT_d.rearrange("(kc p) t -> p kc t", p=128)[:, :, tok0:tok0 + T]
        self.dma(self.sp, self.xld_ch, self.xT[:, :, :T], src, writes=self.xT_b)

    def norm_begin(self, T):
        i = self.bank_i
        while i in self.reserved:
            i = (i + 1) % 8
        self.bank_i = (i + 1) % 8
        self.reserved.add(i)
        self.norm_state = {"bank": i, "n": 0, "T": T}

    def norm_kc(self, kc):
        st = self.norm_state
        T = st["T"]
        ps, psb = self.ps[st["bank"]], self.psb[st["bank"]]
        i = self.talloc()
        self.A(self.tp[i][:, :T], self.xT[:, kc, :T], AF.Square, [self.xT_b[kc]], [self.tp_b[i]])
        self.MM(ps[:, :T], self.ones_f, self.tp[i][:, :T], [self.tp_b[i], self.cst_b], [psb], start=(st["n"] == 0), stop=(st["n"] == KC - 1))
        self.tfree(i)
        st["n"] += 1

    def rstd_of_x(self, T):
        if self.norm_state is None:
            self.norm_begin(T)
            for kc in range(KC):
                self.norm_kc(kc)
        st = self.norm_state
        assert st["n"] == KC and st["T"] == T
        ps, psb = self.ps[st["bank"]], self.psb[st["bank"]]
        i = self.talloc()
        self.A(self.tp[i][:, :T], ps[:, :T], AF.Sqrt, [psb, self.misc_b], [self.tp_b[i]], bias=self.eps_rms[:], scale=1.0 / D)
        self.op(self.dve, lambda e: e.reciprocal(out=self.tp[i][:, :T], in_=self.tp[i][:, :T]), reads=[self.tp_b[i]], writes=[self.tp_b[i]])
        self.reserved.discard(st["bank"])
        self.norm_state = None
        return i

    def rmsnorm(self, gsl, dst, dst_b, T):
        i = self.rstd_of_x(T)
        for kc in range(KC):
            self.Vstt(dst[:, kc, :T], self.xT[:, kc, :T], gsl(kc), self.tp[i][:, :T], ALU.mult, ALU.mult,
                      [self.xT_b[kc], self.tp_b[i], self.vec_b], [dst_b[kc]])
        return i

    def wload(self, parts, deps):
        i = self.wt_i
        self.wt_i = 1 - i
        t, b, ch = self.wt[i], self.wt_b[i], self.wt_ch[i]
        if NODMA and ch.count >= 64:
            return t, b
        for dstf, src in parts:
            self.dma(self.sp, ch, dstf(t), src, reads=[self.cvb[k] for k in deps], writes=[b])
        return t, b

    def proj(self, w3, wb, cols, T, rhs, rhs_b, nk=KC):
        ps, psb = self.bank()
        for kc in range(nk):
            self.MM(ps[:, :T], w3[:, kc, cols], rhs[:, kc, :T], [wb] + rhs_b(kc), [psb], start=(kc == 0), stop=(kc == nk - 1), signal=(kc == nk - 1))
        return ps, psb

    def mlp(self, l, T):
        g2 = lambda kc: self.vec[:, l, VC_N2 + kc:VC_N2 + kc + 1]
        i = self.rmsnorm(g2, self.hT, self.hT_b, T)
        self.tfree(i)
        wup = self.wup_s[l].rearrange("(kc p) n -> p kc n", p=128)
        wdn = self.wdn_s[l].rearrange("(g kc p) n -> g p kc n", p=128, kc=16)
        v3 = lambda t: t[:, 0:8192].rearrange("p (kc n) -> p kc n", kc=16)
        hb = lambda kc: [self.hT_b[kc]]
        for g in range(4):
            for u in range(4):
                c0 = (g * 4 + u) * 512
                wt, wb = self.wload([(v3, wup[:, :, c0:c0 + 512])], [("wup", l, g)])
                w3 = v3(wt)
                for j in range(4):
                    ps, psb = self.proj(w3, wb, slice(j * 128, (j + 1) * 128), T, self.hT, hb)
                    ti = self.talloc()
                    self.A(self.tp[ti][:, :T], ps[:, :T], AF.Relu, [psb], [self.tp_b[ti]])
                    f = u * 4 + j
                    self.Vtt(self.upT[:, f, :T], self.tp[ti][:, :T], self.tp[ti][:, :T], ALU.mult, [self.tp_b[ti]], [self.upT_b[f]])
                    self.tfree(ti)
            if g == 3:
                self.norm_begin(T)
            for u in range(4):
                wt, wb = self.wload([(v3, wdn[g, :, :, u * 512:(u + 1) * 512])], [("wdn", l, g)])
                w3 = v3(wt)
                for j in range(4):
                    o = u * 4 + j
                    ps, psb = self.proj(w3, wb, slice(j * 128, (j + 1) * 128), T, self.upT, lambda kc: [self.upT_b[kc]])
                    self.Vtt(self.xT[:, o, :T], self.xT[:, o, :T], ps[:, :T], ALU.add, [psb, self.xT_b[o]], [self.xT_b[o]])
                    if g == 3 and o >= 1:
                        self.norm_kc(o - 1)
            if g == 3:
                self.norm_kc(KC - 1)

    def mix_shift(self, l, q, ps, psb, segs, T, bnd):
        raw = self.talloc()
        dl = self.talloc()
        mu = self.vec[:, l, VC_MU + q:VC_MU + q + 1]
        R = self.tp[raw]
        for si, (c0, n, _) in enumerate(segs):
            base = c0 + si
            self.A(R[:, base + 1:base + 1 + n], ps[:, c0:c0 + n], AF.Copy, [psb], [self.tp_b[raw]])
            bap, bb = bnd(si)
            self.Vcp(R[:, base:base + 1], bap, bb, [self.tp_b[raw]])
        for si, (c0, n, _) in enumerate(segs):
            base = c0 + si
            self.Vtt(self.tp[dl][:, c0:c0 + n], R[:, base:base + n], R[:, base + 1:base + 1 + n], ALU.subtract, [self.tp_b[raw]], [self.tp_b[dl]])
            self.Vstt(self.tp[dl][:, c0:c0 + n], self.tp[dl][:, c0:c0 + n], mu, R[:, base + 1:base + 1 + n], ALU.mult, ALU.add,
                      [self.tp_b[dl], self.tp_b[raw], self.vec_b], [self.tp_b[dl]])
        if not segs[0][2]["sample"]:
            c0, n, _ = segs[0]
            self.Vcp(self.plast[:, l, q:q + 1], R[:, c0 + n:c0 + n + 1], [self.tp_b[raw]], [self.plast_b])
        self.tfree(raw)
        return dl

    def mixer(self, l, T, segs, ti):
        cfg = self.cfg
        is_s = segs[0][2]["sample"]
        nch = T // CH
        npair = T // 128
        win = self.w_in_s[l].rearrange("(kc p) n -> p kc n", p=128)
        V = lambda c: self.vec[:, l, c:c + 1]
        hb = lambda kc: [self.hT_b[kc]]
        self.dma(self.sp, self.lw_ch, self.lw[0:64, 0:1024], self.w2_s[l], reads=[self.cvb[("w2", l)]], writes=[self.lw_b])
        self.dma(self.sp, self.lw_ch, self.lw[64:128, 0:1024], self.a2_s[l], reads=[self.cvb[("a2", l)]], writes=[self.lw_b])
        self.dma(self.sp, self.lw_ch, self.lw[:, 1024:2048], self.g2_s[l], reads=[self.cvb[("g2", l)]], writes=[self.lw_b])
        g1 = lambda kc: self.vec[:, l, VC_N1 + kc:VC_N1 + kc + 1]
        ri = self.rmsnorm(g1, self.hT, self.hT_b, T)
        for si, (c0, n, info) in enumerate(segs):
            if info["last"]:
                last = c0 + n - 1
                for kc in range(KC):
                    self.Vstt(self.ostage[:, kc:kc + 1], self.xT[:, kc, last:last + 1], g1(kc), self.tp[ri][:, last:last + 1], ALU.mult, ALU.mult,
                              [self.xT_b[kc], self.tp_b[ri], self.vec_b], [self.ost_b])
                self.dma(self.sp, self.ost_ch, self.shifto_d[l, info["seq"]], self.ostage[:, 0:16], reads=[self.ost_b])
        self.tfree(ri)
        if is_s:
            for b in range(2):
                self.Vcp(self.hT[:, :, T + b:T + b + 1], self.hs0[:, l, :, b:b + 1], [self.st_b], [self.hTx_b])

        def shift_bnd(q, w3, wb, cols):
            if not is_s:
                return lambda si: (self.plast[:, l, q:q + 1], [self.plast_b])
            ps, psb = self.bank()
            for kc in range(KC):
                self.MM(ps[:, 0:2], w3[:, kc, cols], self.hT[:, kc, T:T + 2], [wb, self.hTx_b], [psb], start=(kc == 0), stop=(kc == KC - 1), signal=(kc == KC - 1))
            return lambda si: (ps[:, si:si + 1], [psb])

        v32 = lambda t: t[:, 0:8192].rearrange("p (kc j n) -> p kc j n", kc=16, j=2)
        v31 = lambda t: t[:, 0:4096].rearrange("p (kc n) -> p kc n", kc=16)
        win_c = self.w_in_s[l][:, 0:3072].rearrange("(kc p) (j n) -> p kc j n", p=128, j=3)
        for q in range(4):
            wtA, wbA = self.wload([(lambda t: v32(t)[:, :, 0, :], win_c[:, :, 0, q * 256:(q + 1) * 256]),
                                   (lambda t: v32(t)[:, :, 1, :], win_c[:, :, 2, q * 256:(q + 1) * 256])], [("win_c", l)])
            wA = v32(wtA)
            convs = []
            for hf in range(2):
                cg = q * 2 + hf
                cols = slice(hf * 128, (hf + 1) * 128)
                ps_in, pb_in = self.proj(wA[:, :, 0, :], wbA, cols, T, self.hT, hb)
                ps_c, pb_c = self.proj(wA[:, :, 1, :], wbA, cols, T, self.hT, hb)
                ub = self.talloc()
                acc = self.talloc()
                tmp = self.talloc()
                U = self.tp[ub]
                self.A(self.tp[tmp][:, :T], ps_in[:, :T], AF.Copy, [pb_in], [self.tp_b[tmp]])
                for si, (c0, n, info) in enumerate(segs):
                    base = c0 + 2 * si
                    self.Vtt(U[:, base + 2:base + 2 + n], self.tp[tmp][:, c0:c0 + n], ps_c[:, c0:c0 + n], ALU.mult, [self.tp_b[tmp], pb_c], [self.tp_b[ub]])
                    if info["sample"]:
                        self.Vcp(U[:, base:base + 2], self.cv0[:, l, cg, info["b"], :], [self.st2_b], [self.tp_b[ub]])
                    else:
                        self.Vcp(U[:, base:base + 2], self.ulast[:, l, cg, :], [self.ulast_b], [self.tp_b[ub]])
                    cw = lambda j: V(VC_CW + j * 8 + cg)
                    AC = self.tp[acc]
                    self.Vts(AC[:, c0:c0 + n], U[:, base:base + n], cw(0), ALU.mult, [self.tp_b[ub], self.vec_b], [self.tp_b[acc]])
                    self.Vstt(AC[:, c0:c0 + n], U[:, base + 1:base + 1 + n], cw(1), AC[:, c0:c0 + n], ALU.mult, ALU.add, [self.tp_b[ub], self.tp_b[acc], self.vec_b], [self.tp_b[acc]])
                    self.Vstt(AC[:, c0:c0 + n], U[:, base + 2:base + 2 + n], cw(2), AC[:, c0:c0 + n], ALU.mult, ALU.add, [self.tp_b[ub], self.tp_b[acc], self.vec_b], [self.tp_b[acc]])
                    if info["sample"] or True:
                        pass
                    if not info["sample"]:
                        self.Vcp(self.ulast[:, l, cg, :], U[:, base + n:base + n + 2], [self.tp_b[ub]], [self.ulast_b])
                    if info["last"]:
                        ob = 16 + si * 16
                        self.Vcp(self.ostage[:, ob + cg * 2:ob + cg * 2 + 2], U[:, base + n:base + n + 2], [self.tp_b[ub]], [self.ost_b])
                        if cg == 7:
                            self.dma(self.sp, self.ost_ch, self.convo_d[l, info["seq"]], self.ostage[:, ob:ob + 16].rearrange("p (c j) -> p c j", j=2), reads=[self.ost_b])
                self.tfree(ub, tmp)
                convs.append((cg, acc))
            wtB, wbB = self.wload([(v31, win[:, :, 1024 + q * 256:1024 + (q + 1) * 256])], [("win_c", l)])
            wB = v31(wtB)
            for hf, (cg, acc) in enumerate(convs):
                cols = slice(hf * 128, (hf + 1) * 128)
                ps_b, pb_b = self.proj(wB, wbB, cols, T, self.hT, hb)
                self.Vtt(self.ycinT[:, cg, :T], self.tp[acc][:, :T], ps_b[:, :T], ALU.mult, [self.tp_b[acc], pb_b], [self.ycin_b[cg]])
                self.tfree(acc)

        if cfg.do_rwkv:
            self.rwkv(l, T, segs, win, shift_bnd)

        woc = self.woc_s[l].rearrange("(kc p) n -> p kc n", p=128)
        wor = self.wor_s[l].rearrange("(kc p) n -> p kc n", p=128)
        v2x8 = lambda t: t[:, 0:4096].rearrange("p (a kc n) -> p a kc n", a=2, kc=8)
        for q in range(8):
            c0 = q * 256
            wtG, wbG = self.wload([(lambda t: v32(t)[:, :, 0, :], win[:, :, OFF_GC + c0:OFF_GC + c0 + 256]),
                                   (lambda t: v32(t)[:, :, 1, :], win[:, :, OFF_GR + c0:OFF_GR + c0 + 256])], [("win_g", l)])
            wG = v32(wtG)
            wtA, wbA = self.wload([(lambda t: v2x8(t)[:, 0], woc[:, :, c0:c0 + 256]), (lambda t: v2x8(t)[:, 1], wor[:, :, c0:c0 + 256])],
                                  [("woc", l), ("wor", l)])
            wA = v2x8(wtA)
            gates = []
            for hf in range(2):
                cols = slice(hf * 128, (hf + 1) * 128)
                ps_gc, pb_gc = self.proj(wG[:, :, 0, :], wbG, cols, T, self.hT, hb)
                t1 = self.talloc()
                self.A(self.tp[t1][:, :T], ps_gc[:, :T], AF.Sigmoid, [pb_gc], [self.tp_b[t1]])
                t2 = None
                if cfg.do_rwkv:
                    ps_gr, pb_gr = self.proj(wG[:, :, 1, :], wbG, cols, T, self.hT, hb)
                    t2 = self.talloc()
                    self.A(self.tp[t2][:, :T], ps_gr[:, :T], AF.Sigmoid, [pb_gr], [self.tp_b[t2]])
                gates.append((t1, t2))
            for hf in range(2):
                o = q * 2 + hf
                cols = slice(hf * 128, (hf + 1) * 128)
                t1, t2 = gates[hf]
                ps_yc, pb_yc = self.proj(wA[:, 0], wbA, cols, T, self.ycinT, lambda kc: [self.ycin_b[kc]], nk=8)
                if cfg.do_rwkv:
                    ps_yr, pb_yr = self.proj(wA[:, 1], wbA, cols, T, self.yrT, lambda kc: [self.yr_b[kc]], nk=8)
                    self.Vtt(self.tp[t1][:, :T], self.tp[t1][:, :T], ps_yc[:, :T], ALU.mult, [self.tp_b[t1], pb_yc], [self.tp_b[t1]])
                    self.Vtt(self.tp[t2][:, :T], self.tp[t2][:, :T], ps_yr[:, :T], ALU.mult, [self.tp_b[t2], pb_yr], [self.tp_b[t2]])
                    self.Vtt(self.upT[:, o, :T], self.tp[t1][:, :T], self.tp[t2][:, :T], ALU.add, [self.tp_b[t1], self.tp_b[t2]], [self.upT_b[o]])
                    self.tfree(t2)
                else:
                    self.Vtt(self.upT[:, o, :T], self.tp[t1][:, :T], ps_yc[:, :T], ALU.mult, [self.tp_b[t1], pb_yc], [self.upT_b[o]])
                self.tfree(t1)
        wo = self.wo_s[l].rearrange("(kc p) n -> p kc n", p=128)
        v3 = lambda t: t[:, 0:8192].rearrange("p (kc n) -> p kc n", kc=16)
        self.norm_begin(T)
        for u in range(4):
            wt, wb = self.wload([(v3, wo[:, :, u * 512:(u + 1) * 512])], [("wo", l)])
            w3 = v3(wt)
            for j in range(4):
                o = u * 4 + j
                ps, psb = self.proj(w3, wb, slice(j * 128, (j + 1) * 128), T, self.upT, lambda kc: [self.upT_b[kc]])
                self.Vtt(self.xT[:, o, :T], self.xT[:, o, :T], ps[:, :T], ALU.add, [psb, self.xT_b[o]], [self.xT_b[o]])
                if o >= 1:
                    self.norm_kc(o - 1)
        self.norm_kc(KC - 1)

    def run_gens(self, gens):
        gens = [g for g in gens if g is not None]
        reps = GEN_REPS
        while gens:
            for gi, g in enumerate(list(gens)):
                try:
                    for _ in range(reps[gi % len(reps)]):
                        next(g)
                except StopIteration:
                    gens.remove(g)

    def rwkv(self, l, T, segs, win, shift_bnd):
        hb = lambda kc: [self.hT_b[kc]]
        tp, tb = self.tp, self.tp_b
        v31 = lambda t: t[:, 0:4096].rearrange("p (kc n) -> p kc n", kc=16)
        wtL, wbL = self.wload([(v31, win[:, :, OFF_LORA:OFF_LORA + 256])], [("win_r", l)])
        wL = v31(wtL)
        cA = slice(0, 128)
        cB = slice(128, 256)
        psA, pbA = self.proj(wL, wbL, cA, T, self.hT, hb)
        mA = self.mix_shift(l, 24, psA, pbA, segs, T, shift_bnd(24, wL, wbL, cA))
        self.A(self.lwin[0:64, :T], tp[mA][0:64, :T], AF.Tanh, [tb[mA]], [self.lwin_b])
        self.A(self.lwin[64:128, :T], tp[mA][64:128, :T], AF.Copy, [tb[mA]], [self.lwin_b])
        self.tfree(mA)
        psB, pbB = self.proj(wL, wbL, cB, T, self.hT, hb)
        mB = self.mix_shift(l, 25, psB, pbB, segs, T, shift_bnd(25, wL, wbL, cB))
        self.A(self.sgb[:, :T], tp[mB][:, :T], AF.Sigmoid, [tb[mB]], [self.sgb_b])
        self.tfree(mB)
        win_r = self.w_in_s[l][:, OFF_R:OFF_R + 3072].rearrange("(kc p) (j n) -> p kc j n", p=128, j=3)
        prev_tail = None
        pctx = None
        for s in range(8):
            ctx = {"A_done": False, "tm_done": False, "prev": pctx}
            pctx = ctx
            head = self.slab_head(l, s, T, segs, win_r, shift_bnd, ctx)
            self.run_gens([prev_tail, head])
            prev_tail = self.slab_tail(l, s, T, segs, ctx)
            if "tail" in self.cfg.skip:
                ctx["A_done"] = ctx["tm_done"] = True
                self.tfree(ctx["bon"])
                prev_tail = None
        self.run_gens([prev_tail])

    def slab_head(self, l, s, T, segs, win_r, shift_bnd, ctx):
        tp, tb = self.tp, self.tp_b
        SB = self.slab_b
        hb = lambda kc: [self.hT_b[kc]]
        V = lambda c: self.vec[:, l, c + s:c + s + 1]
        nch = T // CH
        npair = T // 128
        par = s % 2
        v33 = lambda t: t[:, 0:6144].rearrange("p (kc j n) -> p kc j n", kc=16, j=3)
        wt, wb = self.wload([((lambda t, j=j: v33(t)[:, :, j, :]), win_r[:, :, j, s * 128:(s + 1) * 128]) for j in range(3)], [("win_r", l)])
        w = v33(wt)
        call = slice(0, 128)
        m = []
        for j in range(3):
            q = j * 8 + s
            ps, pb = self.proj(w[:, :, j, :], wb, call, T, self.hT, hb)
            yield
            m.append(self.mix_shift(l, q, ps, pb, segs, T, shift_bnd(q, w[:, :, j, :], wb, call)))
            yield
        r_f, k_f, v_f = m
        ps_d, pb_d = self.bank()
        self.MM(ps_d[:, :T], self.lw[0:64, s * 128:(s + 1) * 128], self.lwin[0:64, :T], [self.lw_b, self.lwin_b], [pb_d])
        ps_a, pb_a = self.bank()
        self.MM(ps_a[:, :T], self.lw[64:128, s * 128:(s + 1) * 128], self.lwin[64:128, :T], [self.lw_b, self.lwin_b], [pb_a])
        a_f = self.talloc()
        self.A(tp[a_f][:, :T], ps_a[:, :T], AF.Sigmoid, [pb_a, self.vec_b], [tb[a_f]], bias=V(VC_A0))
        sg = self.talloc()
        self.A(tp[sg][:, :T], ps_d[:, :T], AF.Sigmoid, [pb_d, self.vec_b], [tb[sg]], bias=V(VC_W0))
        yield
        cs = self.talloc()
        self.op(self.dve, lambda e: e.tensor_tensor_scan(out=tp[cs][:, :T], data0=self.chmask[:, :T], data1=tp[sg][:, :T], initial=0.0,
                                                         op0=ALU.mult, op1=ALU.add), reads=[tb[sg], self.cst_b], writes=[tb[cs]])
        kkr = self.talloc()
        self.Vts(tp[kkr][:, :T], tp[k_f][:, :T], V(VC_KK), ALU.mult, [tb[k_f], self.vec_b], [tb[kkr]])
        t = self.talloc()
        self.A(tp[t][:, :T], tp[kkr][:, :T], AF.Square, [tb[kkr]], [tb[t]])
        yield
        ps_n, pb_n = self.bank()
        self.MM(ps_n[:, :T], self.bo1, tp[t][:, :T], [tb[t], self.cst_b], [pb_n])
        self.A(tp[t][:, :T], ps_n[:, :T], AF.Sqrt, [pb_n], [tb[t]])
        self.Vts(tp[t][:, :T], tp[t][:, :T], 1e-12, ALU.max, [tb[t]], [tb[t]])
        yield
        self.op(self.dve, lambda e: e.reciprocal(out=tp[t][:, :T], in_=tp[t][:, :T]), reads=[tb[t]], writes=[tb[t]])
        self.Vtt(tp[kkr][:, :T], tp[kkr][:, :T], tp[t][:, :T], ALU.mult, [tb[kkr], tb[t]], [tb[kkr]])
        yield
        self.Vts(tp[t][:, :T], tp[a_f][:, :T], -1.0, ALU.add, [tb[a_f], self.vec_b], [tb[t]], s2=V(VC_KA), op1=ALU.mult)
        kp = self.talloc()
        self.Vstt(tp[kp][:, :T], tp[t][:, :T], 1.0, tp[k_f][:, :T], ALU.add, ALU.mult, [tb[t], tb[k_f]], [tb[kp]])
        self.tfree(k_f)
        yield
        nb = self.talloc()
        self.Vtt(tp[nb][:, :T], tp[kkr][:, :T], tp[a_f][:, :T], ALU.mult, [tb[kkr], tb[a_f]], [tb[nb]])
        self.tfree(a_f)
        self.Vstt(tp[t][:, :T], tp[r_f][:, :T], V(VC_RK), tp[kp][:, :T], ALU.mult, ALU.mult, [tb[r_f], tb[kp], self.vec_b], [tb[t]])
        yield
        ps_b, pb_b = self.bank()
        self.MM(ps_b[:, :T], self.bo1, tp[t][:, :T], [tb[t], self.cst_b], [pb_b])
        bon = t
        self.Vtt(tp[bon][:, :T], tp[v_f][:, :T], ps_b[:, :T], ALU.mult, [tb[v_f], pb_b], [tb[bon]])
        yield
        eg = self.talloc()
        self.A(tp[eg][:, :T], tp[cs][:, :T], AF.Exp, [tb[cs]], [tb[eg]], scale=-DS)
        e = self.talloc()
        self.Vtt(tp[e][:, :T], tp[cs][:, :T], tp[sg][:, :T], ALU.subtract, [tb[cs], tb[sg]], [tb[e]])
        self.A(tp[e][:, :T], tp[e][:, :T], AF.Exp, [tb[e]], [tb[e]], scale=-DS)
        yield
        prev = ctx["prev"]
        while prev is not None and not prev["A_done"]:
            yield
        self.Vtt(self.rq[:, 0, :T], tp[kkr][:, :T], tp[e][:, :T], ALU.mult, [tb[kkr], tb[e]], [SB["rq"]])
        self.Vtt(self.rq[:, 1, :T], tp[r_f][:, :T], tp[eg][:, :T], ALU.mult, [tb[r_f], tb[eg]], [SB["rq"]])
        rhat, rhb = self.rhat2[par], self.rhat2_b[par]
        self.Vtt(rhat[:, :T], tp[r_f][:, :T], tp[eg][:, :T], ALU.mult, [tb[r_f], tb[eg]], [rhb])
        gam, gamb = self.gam2[par], self.gam2_b[par]
        self.Vcp(gam[:, 0:nch], tp[eg][:, :T].rearrange("p (c t) -> p c t", t=CH)[:, :, CH - 1], [tb[eg]], [gamb])
        self.tfree(eg)
        yield
        self.A(tp[e][:, :T], tp[cs][:, :T], AF.Exp, [tb[cs]], [tb[e]], scale=DS)
        self.Vtt(self.kt[:, :T], tp[kp][:, :T], tp[e][:, :T], ALU.mult, [tb[kp], tb[e]], [SB["kt"]])
        self.Vstt(self.bt[:, :T], tp[nb][:, :T], -1.0, tp[e][:, :T], ALU.mult, ALU.mult, [tb[nb], tb[e]], [SB["bt"]])
        yield
        csv = tp[cs][:, :T].rearrange("p (c t) -> p c t", t=CH)
        self.Vtt(tp[e][:, :T].rearrange("p (c t) -> p c t", t=CH), csv[:, :, CH - 1:CH].broadcast_to([128, nch, CH]), csv, ALU.subtract,
                 [tb[cs]], [tb[e]])
        self.A(tp[e][:, :T], tp[e][:, :T], AF.Exp, [tb[e]], [tb[e]], scale=-DS)
        yield
        self.Vtt(self.ktp[:, :T], tp[kp][:, :T], tp[e][:, :T], ALU.mult, [tb[kp], tb[e]], [SB["ktp"]])
        self.Vstt(self.btp[:, :T], tp[nb][:, :T], -1.0, tp[e][:, :T], ALU.mult, ALU.mult, [tb[nb], tb[e]], [SB["btp"]])
        self.A(self.vb[:, :T], tp[v_f][:, :T], AF.Copy, [tb[v_f]], [SB["vb"]])
        self.tfree(e, cs, sg, kkr, kp, nb, r_f, v_f)
        yield
        while prev is not None and not prev["tm_done"]:
            yield
        for pr in range(npair):
            cols = slice(pr * 128, (pr + 1) * 128)
            ps, psb = self.bank()
            pb16 = ps[:].bitcast(BF16)
            srcs = [(self.rq[:, 0, cols], "rq"), (self.btp[:, cols], "btp"), (self.ktp[:, cols], "ktp"), (self.vb[:, cols], "vb")]
            for j, (ap, nm) in enumerate(srcs):
                self.TR(pb16[:, j * 128:(j + 1) * 128], ap, [SB[nm]], [psb], signal=(j == 3))
            self.A(self.tm[pr][:].rearrange("p a b -> p (a b)"), pb16[:, 0:512], AF.Copy, [psb], [self.tm_b[pr]])
            for c in range(2):
                for a_, src_i in ((0, 1), (1, 2)):
                    self.op(self.pool, lambda e, a_=a_, src_i=src_i, c=c: e.tensor_scalar(
                        out=self.tmm[pr][:, a_, c, :], in0=self.tm[pr][:, src_i, :], scalar1=self.rowmask[:, c:c + 1], scalar2=1.0,
                        op0=ALU.mult, op1=ALU.mult), reads=[self.tm_b[pr], self.cst_b], writes=[self.tmm_b[pr]])
            yield
        ctx["bon"] = bon
        ctx["par"] = par

    def slab_tail(self, l, s, T, segs, ctx):
        tp, tb = self.tp, self.tp_b
        SB = self.slab_b
        V = lambda c: self.vec[:, l, c + s:c + s + 1]
        nch = T // CH
        npair = T // 128
        bon = ctx["bon"]
        par = ctx["par"]
        rhat, rhb = self.rhat2[par], self.rhat2_b[par]
        gam, gamb = self.gam2[par], self.gam2_b[par]
        y_f = self.talloc()
        its = [(pr, hh) for pr in range(npair) for hh in range(2)]
        G = len(its)
        GW = min(G, LOCK_W)
        hps = [slice(hh * 64, (hh + 1) * 64) for (_, hh) in its]
        colss = [slice(pr * 128, (pr + 1) * 128) for (pr, _) in its]
        for g0 in range(0, G, GW):
          grp = list(range(g0, min(G, g0 + GW)))
          for i, (pr, hh) in [(ii, its[ii]) for ii in grp]:
              hp, cols = hps[i], colss[i]
              Nm, Nmb = self.NmG[i], self.NmG_b[i]
              ps1, pb1 = self.bank()
              rqv = self.rq[hp, :, cols]
              self.MM(ps1[:, 0:256], self.bt[hp, cols], rqv, [SB["bt"], SB["rq"]], [pb1], signal=False)
              self.MM(ps1[:, 256:512], self.kt[hp, cols], rqv, [SB["kt"], SB["rq"]], [pb1])
              self.Vtt(Nm[:, :], ps1[:, :], self.mask4, ALU.mult, [pb1, self.cst_b], [Nmb])
              yield
          for i, (pr, hh) in [(ii, its[ii]) for ii in grp]:
              hp, cols = hps[i], colss[i]
              XX, XXb = self.XXG[i], self.XXG_b[i]
              tm, tmb = self.tm[pr], self.tm_b[pr]
              ps2, pb2 = self.bank()
              self.MM(ps2[:, 0:128], self.rq[hp, 0, cols], self.bt[hp, cols], [SB["bt"], SB["rq"]], [pb2], signal=False)
              self.MM(ps2[:, 128:192], self.NmG[i][:, 256:384], tm[:, 3, hps[i]], [self.NmG_b[i], tmb], [pb2])
              self.Vtt(XX[:, 0, 1, :], ps2[:, 0:128], self.mls, ALU.mult, [pb2, self.cst_b], [XXb[0]])
              Zb, Zbb = self.ZbG[i], self.ZbG_b[i]
              self.A(Zb[:, 0, 0:64], ps2[:, 128:192], AF.Copy, [pb2], [Zbb[0]])
              self.Vcp(Zb[:, 0, 64:128], tm[:, 0, hps[i]], [tmb], [Zbb[0]])
              yield
          if g0 + GW >= G:
              ctx["A_done"] = True
          for j in range(6):
              banks = []
              for i in grp:
                  Nm, Nmb = self.NmG[i], self.NmG_b[i]
                  XX, XXb = self.XXG[i], self.XXG_b[i]
                  Zb, Zbb = self.ZbG[i], self.ZbG_b[i]
                  sl = j % 2
                  if j == 0:
                      Xj, XTj, xr = Nm[:, 0:128], XX[:, 0, 1, :], [Nmb, XXb[0]]
                  else:
                      Xj, XTj, xr = XX[:, sl, 0, :], XX[:, sl, 1, :], [XXb[sl]]
                  psq, pbq = self.bank()
                  banks.append((psq, pbq))
                  if j < 5:
                      self.MM(psq[:, 0:128], XTj, Xj, xr, [pbq], signal=False)
                      self.MM(psq[:, 128:256], Xj, XTj, xr, [pbq], signal=False)
                  self.MM(psq[:, 256:384], Xj, Zb[:, sl, :], xr + [Zbb[sl]], [pbq])
                  if i % 2 == 1:
                      yield
              for i in grp:
                  XX, XXb = self.XXG[i], self.XXG_b[i]
                  Zb, Zbb = self.ZbG[i], self.ZbG_b[i]
                  sl = j % 2
                  psq, pbq = banks[i - g0]
                  self.Vtt(Zb[:, 1 - sl, :], Zb[:, sl, :], psq[:, 256:384], ALU.add, [Zbb[sl], pbq], [Zbb[1 - sl]])
                  if j < 5:
                      self.A(XX[:, 1 - sl, :, :].rearrange("p a b -> p (a b)"), psq[:, 0:256], AF.Copy, [pbq], [XXb[1 - sl]])
                  if i % 2 == 1:
                      yield
          for i, (pr, hh) in [(ii, its[ii]) for ii in grp]:
              hp, cols = hps[i], colss[i]
              hc = hp
              Nm, Nmb = self.NmG[i], self.NmG_b[i]
              tm, tmb = self.tm[pr], self.tm_b[pr]
              Zf, Zfb = self.ZbG[i][:, 0, :], self.ZbG_b[i][0]
              psr, pbr = self.bank()
              self.MM(psr[hp, 0:128], Zf[:, 64:128], Nm[:, 128:256], [Zfb, Nmb], [pbr], signal=False)
              self.MM(psr[hp, 128:256], Zf[:, 0:64], Nm[:, 128:256], [Zfb, Nmb], [pbr], start=True, stop=False, signal=False)
              self.MM(psr[hp, 128:256], tm[:, 3, hc], Nm[:, 384:512], [tmb, Nmb], [pbr], start=False, stop=True)
              self.Vtt(rhat[hp, cols], rhat[hp, cols], psr[hp, 0:128], ALU.add, [rhb, pbr], [rhb])
              self.A(tp[y_f][hp, cols], psr[hp, 128:256], AF.Copy, [pbr], [tb[y_f]])
              yield
          for i, (pr, hh) in [(ii, its[ii]) for ii in grp]:
              hp, cols = hps[i], colss[i]
              hc = hp
              tm, tmb = self.tm[pr], self.tm_b[pr]
              tmm, tmmb = self.tmm[pr], self.tmm_b[pr]
              Zf, Zfb = self.ZbG[i][:, 0, :], self.ZbG_b[i][0]
              psp, pbp = self.bank()
              self.MM(psp[hp, 0:128], Zf[:, 64:128], tmm[:, 0, :, hc], [Zfb, tmmb], [pbp], signal=False)
              for c in range(2):
                  o = slice(128 + c * 64, 128 + (c + 1) * 64)
                  self.MM(psp[hp, o], tmm[:, 0, c, hc], Zf[:, 0:64], [Zfb, tmmb], [pbp], start=True, stop=False, signal=False)
                  self.MM(psp[hp, o], tmm[:, 1, c, hc], tm[:, 3, hc], [tmb, tmmb], [pbp], start=False, stop=True, signal=(c == 1))
              for c in range(2):
                  chn = pr * 2 + c
                  self.A(self.PTbd[hp, chn, hc], psp[hp, c * 64:(c + 1) * 64], AF.Copy, [pbp], [self.PT_b[chn]])
                  self.Vcp(self.Gs[hp, chn, :], psp[hp, 128 + c * 64:128 + (c + 1) * 64], [pbp], [self.G_b[chn]])
              yield
        ctx["tm_done"] = True
        H = None
        bdv = self.bdmask.rearrange("p (a v) -> p a v", a=2)
        Hbd3 = self.Hbd[:].rearrange("p (a v) -> p a v", a=2)
        for c in range(nch if "chain" not in self.cfg.skip else 0):
            t0 = c * CH
            for (c0, n, info) in segs:
                if c0 <= t0 < c0 + n:
                    break
            if t0 == c0:
                if info["sample"]:
                    b = info["b"]
                    H, Hbuf = self.Hs[:, b, :], self.Hs_b[b]
                    self.dma(self.sp, self.Hs_ch[b], H, self.H0_d[l, b, s], writes=[Hbuf])
                else:
                    H, Hbuf = self.Hst[:, l, s, :], self.H_b[l][s]
                self.A(self.Hb[:, :], H, AF.Copy, [Hbuf], [self.Hb_b])
                self.Vtt(Hbd3, H.unsqueeze(1).broadcast_to([128, 2, 64]), bdv, ALU.mult, [Hbuf, self.cst_b], [self.Hbd_b])
            cc = slice(t0, t0 + CH)
            self.Vstt(self.Hn[:, :], H, gam[:, c:c + 1], self.Gs[:, c, :], ALU.mult, ALU.add, [Hbuf, gamb, self.G_b[c]], [self.Hn_b])
            psh, pbh = self.bank()
            self.MM(psh[:, 0:64], self.PTbd[:, c, :], self.Hb[:, :], [self.PT_b[c], self.Hb_b], [pbh])
            psy, pby = self.bank()
            self.MM(psy[:, 0:64], self.Hbd[:, :], rhat[:, cc], [self.Hbd_b, rhb], [pby])
            yield
            last_of_seg = (t0 + CH == c0 + n)
            if not last_of_seg:
                self.Vtt(self.Hb[:, :], self.Hn[:, :], psh[:, 0:64], ALU.add, [self.Hn_b, pbh], [self.Hb_b])
            self.Vtt(H, self.Hn[:, :], psh[:, 0:64], ALU.add, [self.Hn_b, pbh], [Hbuf])
            if not last_of_seg:
                self.Vtt(Hbd3, H.unsqueeze(1).broadcast_to([128, 2, 64]), bdv, ALU.mult, [Hbuf, self.cst_b], [self.Hbd_b])
            self.Vtt(tp[y_f][:, cc], tp[y_f][:, cc], psy[:, 0:64], ALU.add, [tb[y_f], pby], [tb[y_f]])
            if last_of_seg and info["last"]:
                och = self.wkv_ch[info["seq"]]
                self.dma(self.sp, och, self.wkvo_d[l, info["seq"], s], H, reads=[Hbuf])
            yield
        psm, pbm = self.bank()
        self.MM(psm[:, :T], self.bo1, tp[y_f][:, :T], [tb[y_f], self.cst_b], [pbm])
        yc = self.talloc()
        self.Vstt(tp[yc][:, :T], psm[:, :T], -1.0 / 64, tp[y_f][:, :T], ALU.mult, ALU.add, [pbm, tb[y_f]], [tb[yc]])
        self.A(tp[y_f][:, :T], tp[yc][:, :T], AF.Square, [tb[yc]], [tb[y_f]])
        yield
        psv, pbv = self.bank()
        self.MM(psv[:, :T], self.bo1, tp[y_f][:, :T], [tb[y_f], self.cst_b], [pbv])
        self.A(tp[y_f][:, :T], psv[:, :T], AF.Sqrt, [pbv, self.misc_b], [tb[y_f]], bias=self.eps_gn[:], scale=1.0 / 64)
        yield
        self.op(self.dve, lambda e: e.reciprocal(out=tp[y_f][:, :T], in_=tp[y_f][:, :T]), reads=[tb[y_f]], writes=[tb[y_f]])
        self.Vtt(tp[yc][:, :T], tp[yc][:, :T], tp[y_f][:, :T], ALU.mult, [tb[yc], tb[y_f]], [tb[yc]])
        yield
        self.Vts(tp[yc][:, :T], tp[yc][:, :T], V(VC_LW), ALU.mult, [tb[yc], self.vec_b], [tb[yc]], s2=V(VC_LB), op1=ALU.add)
        self.Vtt(tp[yc][:, :T], tp[yc][:, :T], tp[bon][:, :T], ALU.add, [tb[yc], tb[bon]], [tb[yc]])
        yield
        ps_g, pb_g = self.bank()
        self.MM(ps_g[:, :T], self.lw[:, 1024 + s * 128:1024 + (s + 1) * 128], self.sgb[:, :T], [self.lw_b, self.sgb_b], [pb_g])
        self.Vtt(self.yrT[:, s, :T], tp[yc][:, :T], ps_g[:, :T], ALU.mult, [tb[yc], pb_g], [self.yr_b[s]])
        self.tfree(yc, y_f, bon)

    def final_out(self, tok0, T):
        gf = lambda kc: self.nf[:, kc:kc + 1]
        i = self.rstd_of_x(T)
        dst = self.yT_d.rearrange("(kc p) t -> p kc t", p=128)
        for kc in range(KC):
            self.Vstt(self.xT[:, kc, :T], self.xT[:, kc, :T], gf(kc), self.tp[i][:, :T], ALU.mult, ALU.mult,
                      [self.xT_b[kc], self.tp_b[i], self.nf_b], [self.xT_b[kc]])
        self.tfree(i)
        self.dma(self.sp, self.yst_ch, dst[:, :, tok0:tok0 + T], self.xT[:, :, :T], reads=self.xT_b)

    def finish(self):
        for ch in self.out_chans:
            if ch.count:
                self.sp.h.wait_ge(ch.sem, ch.count)

    def build(self):
        cfg = self.cfg
        self.alloc()
        self.setup()
        tiles = []
        for i in range(cfg.npt):
            tiles.append((i * TT, TT, [(0, TT, dict(sample=False, first=(i == 0), last=(i == cfg.npt - 1), seq=0, b=0))]))
        if cfg.sample:
            tiles.append((cfg.npt * TT, 128, [(0, 64, dict(sample=True, first=True, last=True, seq=1, b=0)),
                                             (64, 64, dict(sample=True, first=True, last=True, seq=2, b=1))]))
        for ti, (tok0, T, segs) in enumerate(tiles):
            self.load_x(tok0, T)
            for l in range(cfg.depth):
                if cfg.do_mix:
                    self.mixer(l, T, segs, ti)
                if cfg.do_mlp:
                    self.mlp(l, T)
            self.final_out(tok0, T)
        self.finish()
        return self.nc


C_ID, C_ONES, C_BO, C_M4, C_MLS, C_CHM, C_ROWM, C_BDM = 0, 128, 256, 384, 896, 1024, 1536, 1538
NCST = 1666


def host_consts():
    c = np.zeros((128, NCST), np.float32)
    p = np.arange(128)
    c[:, C_ID:C_ID + 128] = np.eye(128, dtype=np.float32)
    c[:, C_ONES:C_ONES + 128] = 1.0
    same = (p[:, None] // 64) == (p[None, :] // 64)
    c[:, C_BO:C_BO + 128] = same
    mus = same & (p[:, None] < p[None, :])
    mui = same & (p[:, None] <= p[None, :])
    c[:, C_M4:C_M4 + 512] = np.concatenate([mus, mui, mus, mui], 1)
    c[:, C_MLS:C_MLS + 128] = same & (p[:, None] > p[None, :])
    cm = np.ones(512, np.float32)
    cm[::64] = 0.0
    c[:, C_CHM:C_CHM + 512] = cm[None, :]
    c[:, C_ROWM + 0] = (p < 64)
    c[:, C_ROWM + 1] = (p >= 64)
    bd = np.zeros((128, 2, 64), np.float32)
    bd[:64, 0, :] = 1.0
    bd[64:, 1, :] = 1.0
    c[:, C_BDM:C_BDM + 128] = bd.reshape(128, 128)
    return c


def host_vec(inp, L):
    v = np.zeros((L, 128, NVC), np.float32)
    fm = lambda a, n: np.asarray(a, np.float32).reshape(n, 128).T
    for l in range(L):
        v[l, :, VC_N1:VC_N1 + 16] = fm(inp["norm1"][l], 16)
        v[l, :, VC_N2:VC_N2 + 16] = fm(inp["norm2"][l], 16)
        v[l, :, VC_MU:VC_MU + 26] = fm(inp["mu_shift"][l], 26)
        for j in range(3):
            v[l, :, VC_CW + j * 8:VC_CW + j * 8 + 8] = fm(inp["conv_w"][l, j], 8)
        v[l, :, VC_W0:VC_W0 + 8] = fm(inp["w0"][l], 8)
        v[l, :, VC_A0:VC_A0 + 8] = fm(inp["a0"][l], 8)
        v[l, :, VC_KK:VC_KK + 8] = fm(inp["k_k"][l], 8)
        v[l, :, VC_KA:VC_KA + 8] = fm(inp["k_a"][l], 8)
        v[l, :, VC_RK:VC_RK + 8] = fm(np.asarray(inp["r_k"][l]).reshape(-1), 8)
        v[l, :, VC_LW:VC_LW + 8] = fm(inp["ln_x_w"][l], 8)
        v[l, :, VC_LB:VC_LB + 8] = fm(inp["ln_x_b"][l], 8)
    return v


def make_in_map(inp, cfg, xp, xs, cconv, sshift, swkv):
    L = cfg.depth
    m = {"vec": host_vec(inp, L), "nf": np.ascontiguousarray(np.asarray(inp["norm_f"], np.float32).reshape(16, 128).T),
         "cst": host_consts()}
    for k in ("w_in", "w2", "a2", "g2", "w_out_conv", "w_out_rwkv", "w_o", "w_up", "w_down"):
        m[k] = np.ascontiguousarray(np.asarray(inp[k], np.float32)[:L])
    m.update(make_in_map_states(cfg, xp, xs, cconv, sshift, swkv))
    return m


def make_in_map_states(cfg, xp, xs, cconv, sshift, swkv):
    L = cfg.depth
    m = {}
    toks = [np.asarray(xp, np.float32)]
    if cfg.sample:
        toks.append(np.asarray(xs, np.float32).reshape(128, D))
    m["xT"] = np.ascontiguousarray(np.concatenate(toks, 0).T)
    hs0 = np.zeros((L, 128, 16, 2), np.float32)
    cv0 = np.zeros((L, 128, 8, 2, 2), np.float32)
    H0 = np.zeros((L, 2, 8, 128, 64), np.float32)
    if cfg.sample:
        ss = np.asarray(sshift, np.float32)[:L]
        hs0[:] = ss.reshape(L, 2, 16, 128).transpose(0, 3, 2, 1)
        cc = np.asarray(cconv, np.float32)[:L]
        cv0[:] = cc.reshape(L, 2, 2, 8, 128).transpose(0, 4, 3, 1, 2)
        sw = np.asarray(swkv, np.float32)[:L]
        H0[:] = sw.reshape(L, 2, 8, 2, 64, 64).transpose(0, 1, 2, 3, 5, 4).reshape(L, 2, 8, 128, 64)
    m["hs0"], m["cv0"], m["H0"] = hs0, cv0, H0
    return m


_NC_CACHE = {}


def kernel(**inp):
    cfg = Cfg(npt=8, depth=4, sample=True)
    n = 8
    xprompt = np.asarray(inp["x_prompt"], np.float32)
    xsample = np.asarray(inp["x_sample"], np.float32)
    cconv = np.asarray(inp["cache_conv"], np.float32)
    sshift = np.asarray(inp["state_shift"], np.float32)
    swkv = np.asarray(inp["state_wkv"], np.float32)
    kb = KB(cfg)
    nc = kb.build()
    base = make_in_map(inp, cfg, xprompt[0], xsample[0:2], cconv[:, 0:2], sshift[:, 0:2], swkv[:, 0:2])
    in_maps = []
    for c in range(n):
        m = dict(base)
        if c > 0:
            mc = make_in_map_states(cfg, xprompt[c % 4], xsample[2 * c:2 * c + 2], cconv[:, 2 * c:2 * c + 2], sshift[:, 2 * c:2 * c + 2], swkv[:, 2 * c:2 * c + 2])
            m.update(mc)
        in_maps.append(m)
    res = run_bass_kernel_spmd(nc, in_maps, core_ids=list(range(n)))
    L = cfg.depth
    y_p = np.zeros((4, 4096, D), np.float32)
    y_s = np.zeros((16, 64, D), np.float32)
    conv_p = np.zeros((L, 4, 2, 1024), np.float32)
    shift_p = np.zeros((L, 4, D), np.float32)
    wkv_p = np.zeros((L, 4, 16, 64, 64), np.float32)
    conv_s = np.zeros((L, 16, 2, 1024), np.float32)
    shift_s = np.zeros((L, 16, D), np.float32)
    wkv_s = np.zeros((L, 16, 16, 64, 64), np.float32)
    fc = lambda a: a.transpose(0, 3, 2, 1).reshape(L, 2, 1024)
    fs = lambda a: a.transpose(0, 2, 1).reshape(L, D)
    fw = lambda a: a.reshape(L, 8, 2, 64, 64).transpose(0, 1, 2, 4, 3).reshape(L, 16, 64, 64)
    for c in range(n):
        r = res.results[c]
        yT = np.asarray(r["yT"])
        convo, shifto, wkvo = np.asarray(r["convo"]), np.asarray(r["shifto"]), np.asarray(r["wkvo"])
        if c < 4:
            y_p[c] = yT[:, :4096].T
            conv_p[:, c] = fc(convo[:, 0])
            shift_p[:, c] = fs(shifto[:, 0])
            wkv_p[:, c] = fw(wkvo[:, 0])
        y_s[2 * c:2 * c + 2] = yT[:, 4096:].T.reshape(2, 64, D)
        for j in range(2):
            conv_s[:, 2 * c + j] = fc(convo[:, 1 + j])
            shift_s[:, 2 * c + j] = fs(shifto[:, 1 + j])
            wkv_s[:, 2 * c + j] = fw(wkvo[:, 1 + j])
    return (y_p, y_s, conv_p, shift_p, wkv_p, conv_s, shift_s, wkv_s)
```
